# Optimizing a Trainium2 kernel written in Bass

```python
import jax, jax.numpy as jnp
from jax import lax
import numpy as np

D_MODEL = 1024
BATCH = 8
SEQ = 2048
DEPTH = 2

HEAD_DIM = 64
N_HEADS_A = D_MODEL // HEAD_DIM
WIDTH_A = N_HEADS_A * HEAD_DIM
N_Q_HEADS_B = D_MODEL // HEAD_DIM
N_KV_HEADS_B = 4
GROUP_B = N_Q_HEADS_B // N_KV_HEADS_B
WIDTH_B = N_Q_HEADS_B * HEAD_DIM
KV_WIDTH_B = N_KV_HEADS_B * HEAD_DIM
WINDOW = 128
Q_BLOCK = 128
ROT_DIM = HEAD_DIM // 4
ROPE_THETA = 500000.0
EPS = 1e-6
N_A_LAYERS = DEPTH // 2
N_B_LAYERS = DEPTH - N_A_LAYERS

kernel_name = "yoco_fox_swa_sink_hybrid"


def rmsnorm(x, g):
    xf = x.astype(jnp.float32)
    y = xf * lax.rsqrt(jnp.mean(xf * xf, axis=-1, keepdims=True) + EPS)
    return (y * g.astype(jnp.float32)).astype(x.dtype)


def partial_rope(x, positions):
    x_rot, x_pass = x[..., :ROT_DIM], x[..., ROT_DIM:]
    half = ROT_DIM // 2
    inv_freq = jnp.power(jnp.float32(ROPE_THETA), -jnp.arange(0, ROT_DIM, 2, dtype=jnp.float32) / ROT_DIM)
    ang = positions.astype(jnp.float32)[:, None] * inv_freq[None, :]
    cos = jnp.cos(ang)[None, :, None, :]
    sin = jnp.sin(ang)[None, :, None, :]
    xr = x_rot.astype(jnp.float32)
    x1, x2 = xr[..., :half], xr[..., half:]
    rot = jnp.concatenate([x1 * cos - x2 * sin, x1 * sin + x2 * cos], axis=-1)
    return jnp.concatenate([rot.astype(x.dtype), x_pass], axis=-1)


def fox_attention(q, k, v, log_f):
    b, s, h, d = q.shape
    nb = s // Q_BLOCK
    scale = HEAD_DIM ** -0.5
    c = jnp.cumsum(log_f, axis=1)
    c_k = jnp.transpose(c, (0, 2, 1))
    q_blocks = jnp.moveaxis(q.reshape(b, nb, Q_BLOCK, h, d), 1, 0)
    c_blocks = jnp.moveaxis(c_k.reshape(b, h, nb, Q_BLOCK), 2, 0)
    k_pos = jnp.arange(s)

    def block(args):
        idx, qi, ci = args
        logits = jnp.einsum('bqhd,bkhd->bhqk', qi, k, preferred_element_type=jnp.float32) * scale
        logits = logits + ci[..., :, None] - c_k[..., None, :]
        q_pos = idx * Q_BLOCK + jnp.arange(Q_BLOCK)
        causal = k_pos[None, :] <= q_pos[:, None]
        logits = jnp.where(causal, logits, -jnp.inf)
        p = jax.nn.softmax(logits, axis=-1)
        return jnp.einsum('bhqk,bkhd->bqhd', p.astype(v.dtype), v)

    out = lax.map(block, (jnp.arange(nb), q_blocks, c_blocks))
    return jnp.moveaxis(out, 0, 1).reshape(b, s, h, d)


def swa_sink_attention(q, k, v, sinks):
    b, s, hq, d = q.shape
    hkv = k.shape[2]
    g = hq // hkv
    nb = s // WINDOW
    scale = HEAD_DIM ** -0.5
    qb = q.reshape(b, nb, WINDOW, hkv, g, d)
    pad = ((0, 0), (WINDOW, 0), (0, 0), (0, 0))
    kb = jnp.pad(k, pad).reshape(b, nb + 1, WINDOW, hkv, d)
    vb = jnp.pad(v, pad).reshape(b, nb + 1, WINDOW, hkv, d)
    k_band = jnp.concatenate([kb[:, :-1], kb[:, 1:]], axis=2)
    v_band = jnp.concatenate([vb[:, :-1], vb[:, 1:]], axis=2)
    logits = jnp.einsum('bnqhgd,bnkhd->bnhgqk', qb, k_band, preferred_element_type=jnp.float32) * scale
    diff = (jnp.arange(WINDOW)[:, None] + WINDOW) - jnp.arange(2 * WINDOW)[None, :]
    in_window = (diff >= 0) & (diff < WINDOW)
    k_abs = jnp.arange(nb)[:, None] * WINDOW + jnp.arange(2 * WINDOW)[None, :] - WINDOW
    valid = in_window[None] & (k_abs >= 0)[:, None, :]
    logits = jnp.where(valid[None, :, None, None], logits, -jnp.inf)
    sink = jnp.broadcast_to(sinks.astype(jnp.float32).reshape(1, 1, hkv, g, 1, 1), logits.shape[:-1] + (1,))
    probs = jax.nn.softmax(jnp.concatenate([logits, sink], axis=-1), axis=-1)[..., :-1]
    out = jnp.einsum('bnhgqk,bnkhd->bnqhgd', probs.astype(v.dtype), v_band)
    return out.reshape(b, s, hq, d)


def setup_inputs(seed: int = 0) -> dict:
    key = jax.random.key(seed)
    ks = jax.random.split(key, 20)
    f32 = jnp.float32
    in_a = 3 * WIDTH_A + N_HEADS_A + WIDTH_A
    in_b = WIDTH_B + WIDTH_B
    return {
        "x": jax.random.normal(ks[0], (BATCH, SEQ, D_MODEL), f32),
        "positions": jnp.arange(SEQ, dtype=jnp.int32),
        "norm_a_g": 1.0 + 0.02 * jax.random.normal(ks[1], (N_A_LAYERS, D_MODEL), f32),
        "w_in_a": jax.random.normal(ks[2], (N_A_LAYERS, D_MODEL, in_a), f32) * D_MODEL ** -0.5,
        "b_forget": 3.0 + 0.1 * jax.random.normal(ks[3], (N_A_LAYERS, N_HEADS_A), f32),
        "qnorm_a_g": 1.0 + 0.02 * jax.random.normal(ks[4], (N_A_LAYERS, HEAD_DIM), f32),
        "knorm_a_g": 1.0 + 0.02 * jax.random.normal(ks[5], (N_A_LAYERS, HEAD_DIM), f32),
        "w_out_a": jax.random.normal(ks[6], (N_A_LAYERS, WIDTH_A, D_MODEL), f32) * WIDTH_A ** -0.5,
        "kv_norm_g": 1.0 + 0.02 * jax.random.normal(ks[7], (D_MODEL,), f32),
        "w_kv": jax.random.normal(ks[8], (D_MODEL, 2 * KV_WIDTH_B), f32) * D_MODEL ** -0.5,
        "knorm_b_g": 1.0 + 0.02 * jax.random.normal(ks[9], (HEAD_DIM,), f32),
        "norm_b_g": 1.0 + 0.02 * jax.random.normal(ks[10], (N_B_LAYERS, D_MODEL), f32),
        "w_in_b": jax.random.normal(ks[11], (N_B_LAYERS, D_MODEL, in_b), f32) * D_MODEL ** -0.5,
        "qnorm_b_g": 1.0 + 0.02 * jax.random.normal(ks[12], (N_B_LAYERS, HEAD_DIM), f32),
        "sinks": 0.5 * jax.random.normal(ks[13], (N_B_LAYERS, N_Q_HEADS_B), f32),
        "w_out_b": jax.random.normal(ks[14], (N_B_LAYERS, WIDTH_B, D_MODEL), f32) * WIDTH_B ** -0.5,
    }


def reference(x, positions, norm_a_g, w_in_a, b_forget, qnorm_a_g, knorm_a_g, w_out_a,
              kv_norm_g, w_kv, knorm_b_g, norm_b_g, w_in_b, qnorm_b_g, sinks, w_out_b):
    b, s, _ = x.shape
    h = x
    k_shared = None
    v_shared = None
    for layer in range(DEPTH):
        if layer < N_A_LAYERS:
            i = layer
            u = rmsnorm(h, norm_a_g[i])
            proj = u @ w_in_a[i]
            q, k, v, f_logit, gate = jnp.split(
                proj, [WIDTH_A, 2 * WIDTH_A, 3 * WIDTH_A, 3 * WIDTH_A + N_HEADS_A], axis=-1)
            q = rmsnorm(q.reshape(b, s, N_HEADS_A, HEAD_DIM), qnorm_a_g[i])
            k = rmsnorm(k.reshape(b, s, N_HEADS_A, HEAD_DIM), knorm_a_g[i])
            v = v.reshape(b, s, N_HEADS_A, HEAD_DIM)
            log_f = jax.nn.log_sigmoid((f_logit + b_forget[i]).astype(jnp.float32))
            o = fox_attention(q, k, v, log_f).reshape(b, s, WIDTH_A)
            h = h + (o * jax.nn.silu(gate)) @ w_out_a[i]
        else:
            if layer == N_A_LAYERS:
                u_kv = rmsnorm(h, kv_norm_g)
                k_s, v_s = jnp.split(u_kv @ w_kv, [KV_WIDTH_B], axis=-1)
                k_shared = partial_rope(rmsnorm(k_s.reshape(b, s, N_KV_HEADS_B, HEAD_DIM), knorm_b_g), positions)
                v_shared = v_s.reshape(b, s, N_KV_HEADS_B, HEAD_DIM)
            j = layer - N_A_LAYERS
            u = rmsnorm(h, norm_b_g[j])
            q, gate = jnp.split(u @ w_in_b[j], [WIDTH_B], axis=-1)
            q = partial_rope(rmsnorm(q.reshape(b, s, N_Q_HEADS_B, HEAD_DIM), qnorm_b_g[j]), positions)
            o = swa_sink_attention(q, k_shared, v_shared, sinks[j]).reshape(b, s, WIDTH_B)
            h = h + (o * jax.nn.silu(gate)) @ w_out_b[j]
    return h
```

```python
import numpy as np
import concourse.bass as bass
import concourse.mybir as mybir
from concourse.bass_utils import run_bass_kernel_spmd

F32 = mybir.dt.float32
BF16 = mybir.dt.bfloat16
I32 = mybir.dt.int32
ALU = mybir.AluOpType
AF = mybir.ActivationFunctionType
AX = mybir.AxisListType

S = 2048
D = 1024
NT = 16
NKC = 8
HD = 64
EPS = 1e-6
NEG = -30000.0
ROPE_THETA = 500000.0
TWO_PI = 6.283185307179586
PI = 3.141592653589793


class Prog:
    def __init__(self, nc):
        self.nc = nc
        self.eng = {"pe": nc.tensor, "act": nc.scalar, "dve": nc.vector, "pool": nc.gpsimd, "sp": nc.sync}
        self.insts = {e: [] for e in self.eng}
        self.clock = {e: {} for e in self.eng}
        self.snap = {}
        self.lastw = {}
        self.readers = {}
        self.dma_cnt = {}
        self.last_ticket = {}
        self.ever_written = set()
        self.read_before_write = set()

    def _add(self, eng, fn, R, W, dma=None, extra_waits=()):
        clk = self.clock[eng]
        deps = []
        for r in R:
            if r not in self.ever_written:
                self.read_before_write.add(r)
            excl = isinstance(r, tuple) and r[0] == "ps"
            w = self.lastw.get(r)
            if w is not None:
                deps.append((w, True))
            if excl:
                for rd in self.readers.get(r, ()):
                    deps.append((rd, False))
        for wk in W:
            w = self.lastw.get(wk)
            if w is not None:
                deps.append((w, False))
            for rd in self.readers.get(wk, ()):
                deps.append((rd, False))
        for t in extra_waits:
            deps.append((t, True))
        waits = {}
        for (t, raw) in deps:
            te, ti = t
            if te == eng and eng != "pool" and (not raw or eng in ("pe", "sp")):
                continue
            if clk.get(te, 0) >= ti:
                continue
            if waits.get(te, 0) < ti:
                waits[te] = ti
        for te, ti in waits.items():
            for k, v in self.snap[(te, ti)].items():
                if clk.get(k, 0) < v:
                    clk[k] = v
        if dma is not None:
            n = self.dma_cnt.get(dma, 0) + 1
            self.dma_cnt[dma] = n
            ticket = (("dma", dma), n)
        else:
            ticket = (eng, len(self.insts[eng]) + 1)
        sn = dict(clk)
        sn[ticket[0]] = ticket[1]
        self.snap[ticket] = sn
        self.insts[eng].append({"fn": fn, "waits": sorted(waits.items(), key=str), "ticket": ticket, "dma": dma})
        if fn is not None:
            self.last_ticket[ticket[0]] = ticket
        for r in R:
            excl = isinstance(r, tuple) and r[0] == "ps"
            if excl:
                self.lastw[r] = ticket
                self.readers[r] = []
            else:
                self.readers.setdefault(r, []).append(ticket)
        for wk in W:
            self.lastw[wk] = ticket
            self.readers[wk] = []
            self.ever_written.add(wk)
        for r in R:
            if isinstance(r, tuple) and r[0] == "ps":
                self.ever_written.add(r)
        return ticket

    def op(self, eng, fn, R=(), W=()):
        return self._add(eng, fn, R, W)

    def dma(self, fn, R, W, sem, queue="sp"):
        return self._add(queue, fn, R, W, dma=sem)

    def barrier(self):
        tickets = [t for t in self.last_ticket.values()]
        for e in ("pe", "act", "dve", "pool", "sp"):
            ws = [t for t in tickets if t[0] != e]
            self._add(e, None, (), (), extra_waits=ws)
        self.lastw = {}
        self.readers = {}

    def emit(self):
        nc = self.nc
        sems = {}
        for e in ("pe", "act", "dve", "pool"):
            sems[e] = nc.alloc_semaphore("s_" + e)
        for name in self.dma_cnt:
            sems[("dma", name)] = nc.alloc_semaphore("d_" + name)
        signaled = set()
        for e, lst in self.insts.items():
            for ins in lst:
                for te, ti in ins["waits"]:
                    signaled.add((te, ti))
        val = {}
        for e, lst in self.insts.items():
            c = 0
            for ins in lst:
                t = ins["ticket"]
                if t[0] == e and t in signaled:
                    if ins["fn"] is None:
                        raise RuntimeError("barrier nop cannot signal")
                    c += 1
                    val[t] = c
        for e, lst in self.insts.items():
            h = self.eng[e]
            for ins in lst:
                for te, ti in ins["waits"]:
                    if isinstance(te, tuple):
                        h.wait_ge(sems[te], 16 * ti)
                    else:
                        h.wait_ge(sems[te], val[(te, ti)])
                if ins["fn"] is None:
                    continue
                bi = ins["fn"]()
                t = ins["ticket"]
                if ins["dma"] is not None:
                    bi.then_inc(sems[t[0]], 16)
                elif t in signaled:
                    bi.then_inc(sems[e], 1)


def call(f, *a, **k):
    return lambda: f(*a, **k)


class Mem:
    def __init__(self, nc, base, limit):
        self.nc, self.off, self.limit, self.n = nc, base, limit, 0

    def alloc(self, name, shape, dtype):
        esz = 4 if dtype in (F32, I32) else 2
        nbytes = esz
        for s in shape[1:]:
            nbytes *= s
        off = (self.off + 63) // 64 * 64
        if off + nbytes > self.limit:
            raise RuntimeError(f"SBUF overflow allocating {name}: {off}+{nbytes} > {self.limit}")
        self.n += 1
        t = self.nc.alloc_sbuf_tensor_at(f"{name}_{self.n}", list(shape), dtype, offset=off)
        self.off = off + nbytes
        return t


class Rot:
    def __init__(self, items):
        self.items, self.i = list(items), 0

    def next(self):
        x = self.items[self.i % len(self.items)]
        self.i += 1
        return x


HH_ORDER = (0, 1)
PRM_GA, PRM_GKV, PRM_GB, PRM_GQ0, PRM_GK0, PRM_BF, PRM_GQ1, PRM_GK1, PRM_INVF, PRM_SIGN, PRM_M1, PRM_M2, PRM_SINK, PRM_N = 0, 8, 16, 24, 25, 26, 27, 28, 29, 30, 31, 32, 33, 49


def build(dbg=None):
    nc = bass.Bass("TRN2", target_bir_lowering=False)
    dram = {}

    def din(name, shape, dt=F32):
        dram[name] = nc.dram_tensor(name, list(shape), dt, kind="ExternalInput").ap()
        return dram[name]

    x_d = din("x", [S, D])
    pos_d = din("pos", [128, NT], I32)
    prm_d = din("prm", [128, PRM_N])
    wA_hp = din("wA_hp", [8, 128, NKC, 512])
    wA_f = din("wA_f", [128, NKC, 16])
    wA_o = din("wA_o", [128, NKC, D])
    wB_kv = din("wB_kv", [4, 128, NKC, 192])
    posb_d = din("posb", [128, S], I32)
    wB_q = din("wB_q", [128, NKC, D])
    wB_g = din("wB_g", [128, NKC, D])
    wB_o = din("wB_o", [128, NKC, D])
    out_d = nc.dram_tensor("out", [S, D], F32, kind="ExternalOutput").ap()

    P = Prog(nc)
    V, A, T, G, SP = nc.vector, nc.scalar, nc.tensor, nc.gpsimd, nc.sync
    mem = Mem(nc, 16640, 229376)
    psum = nc.alloc_psum_tensor("psum", [128, 8 * 512], F32)

    def bank(b, n=512, p0=0, p1=128, c0=0):
        return psum[p0:p1, b * 512 + c0:b * 512 + c0 + n]

    def bank_bf(b):
        return psum[:, b * 512:(b + 1) * 512].bitcast(BF16)

    ident = mem.alloc("ident", [128, 128], BF16)
    identf = mem.alloc("identf", [128, 128], F32)
    bd = mem.alloc("bd", [128, 128], BF16)
    maskC = mem.alloc("maskC", [128, 128], BF16)
    maskW = mem.alloc("maskW", [128, 128], BF16)
    mtmp = mem.alloc("mtmp", [128, 128], F32)
    prm = mem.alloc("prm", [128, PRM_N], F32)
    epsT = mem.alloc("epsT", [128, 1], F32)
    gq0s = mem.alloc("gq0s", [128, 1], F32)
    negb = mem.alloc("negb", [128, 1], F32)
    small = mem.alloc("small", [128, 64], F32)
    posi = mem.alloc("posi", [128, NT], I32)
    es_bc = mem.alloc("es_bc", [128, 16], F32)
    gq1s = mem.alloc("gq1s", [128, 1], F32)
    wreg0 = mem.off
    WS = [mem.alloc(f"WS{i}", [128, NKC, 512], F32) for i in range(2)]
    WB = [mem.alloc(f"WB{i}", [128, NKC, 512], BF16) for i in range(2)]
    XS = [mem.alloc(f"XS{i}", [128, D], F32) for i in range(2)]
    wreg1 = mem.off
    l0_base = mem.off


    def dump(ap, row0, n, parts=128):
        P.barrier()
        P.op("dve", call(V.tensor_copy, XS[0][0:parts, 0:n], ap), W=["dbgxs"])
        P.dma(call(SP.dma_start, out=out_d[row0:row0 + parts, 0:n], in_=XS[0][0:parts, 0:n]), R=["dbgxs"], W=["dbgout"], sem="outst")
        P.barrier()

    def finish():
        P.barrier()
        P.emit()
        return nc

    def mk_consts():
        P.op("pool", call(G.memset, mtmp[:], 0.0), W=["mtmp"])
        P.op("pool", call(G.affine_select, out=mtmp[:], in_=mtmp[:], compare_op=ALU.not_equal, fill=1.0,
                                             base=0, pattern=[[-1, 128]], channel_multiplier=1), R=["mtmp"], W=["mtmp"])
        P.op("pool", call(G.tensor_copy, ident[:], mtmp[:]), R=["mtmp"], W=["ident"])
        P.op("pool", call(G.tensor_copy, identf[:], mtmp[:]), R=["mtmp"], W=["identf"])
        P.op("pool", call(G.memset, mtmp[:], 0.0), R=["mtmp"], W=["mtmp"])
        P.op("pool", call(G.affine_select, out=mtmp[:], in_=mtmp[:], compare_op=ALU.is_ge, fill=NEG,
                                             base=0, pattern=[[1, 128]], channel_multiplier=-1), R=["mtmp"], W=["mtmp"])
        P.op("pool", call(G.tensor_copy, maskC[:], mtmp[:]), R=["mtmp"], W=["maskC"])
        P.op("pool", call(G.memset, mtmp[:], 0.0), R=["mtmp"], W=["mtmp"])
        P.op("pool", call(G.affine_select, out=mtmp[:], in_=mtmp[:], compare_op=ALU.is_gt, fill=NEG,
                                             base=0, pattern=[[-1, 128]], channel_multiplier=1), R=["mtmp"], W=["mtmp"])
        P.op("pool", call(G.tensor_copy, maskW[:], mtmp[:]), R=["mtmp"], W=["maskW"])
        P.op("pool", call(G.memset, bd[:], 0.0), W=["bd"])
        P.op("pool", call(G.memset, bd[0:64, 0:64], 1.0 / 64), W=["bd"])
        P.op("pool", call(G.memset, bd[64:128, 64:128], 1.0 / 64), W=["bd"])
        P.op("pool", call(G.memset, epsT[:], EPS), W=["epsT"])
        P.dma(call(SP.dma_start, out=prm[:], in_=prm_d[:, :]), R=[], W=["prm"], sem="prm")
        P.dma(call(SP.dma_start, out=posi[:], in_=pos_d[:, :]), R=[], W=["posi"], sem="posi")
        P.op("dve", call(V.tensor_single_scalar, gq0s[:], prm[:, PRM_GQ0:PRM_GQ0 + 1], 0.125, ALU.mult),
             R=["prm"], W=["gq0s"])
        P.op("dve", call(V.tensor_single_scalar, negb[:], prm[:, PRM_BF:PRM_BF + 1], -1.0, ALU.mult),
             R=["prm"], W=["negb"])

    mk_consts()

    wrot = {"ws": 0, "wb": 0}

    def w_dma(src_ap, ncols, nkc=NKC, ws_list=None):
        ws_list = ws_list or WS
        si = wrot["ws"] % len(ws_list)
        wrot["ws"] += 1
        ws = ws_list[si]
        kws = ("ws", si)
        P.dma(call(SP.dma_start, out=ws[:, 0:nkc, 0:ncols], in_=src_ap), R=[], W=[kws], sem=f"ws{si}")
        return (ws, kws, ncols, nkc)

    def w_cast(h, gain_col, wb_list=None):
        ws, kws, ncols, nkc = h
        wb_list = wb_list or WB
        bi = wrot["wb"] % len(wb_list)
        wrot["wb"] += 1
        wb = wb_list[bi]
        kwb = ("wb", bi)
        if gain_col is None:
            half = nkc // 2
            P.op("dve", call(V.tensor_copy, wb[:, 0:half, 0:ncols], ws[:, 0:half, 0:ncols]), R=[kws], W=[kwb])
            P.op("dve", call(V.tensor_copy, wb[:, half:nkc, 0:ncols], ws[:, half:nkc, 0:ncols]), R=[kws], W=[kwb])
        else:
            for kc in range(nkc):
                P.op("dve", call(V.tensor_single_scalar, wb[:, kc, 0:ncols], ws[:, kc, 0:ncols],
                                 prm[:, gain_col + kc:gain_col + kc + 1], ALU.mult),
                     R=[kws, "prm"], W=[kwb])
        return wb, kwb

    def load_w(src_ap, ncols, gain_col, nkc=NKC, ws_list=None, wb_list=None):
        return w_cast(w_dma(src_ap, ncols, nkc, ws_list), gain_col, wb_list)

    def norm_transpose(src_fn, uT, uT_key, ubuf, sqbuf, banks):
        rb = Rot(banks)
        for t in range(NT):
            xap, xkey = src_fn(t)
            u = ubuf[t % 2]
            ku = ("u", t % 2)
            ss = small[:, (t % 2) * 4:(t % 2) * 4 + 1]
            rs = small[:, (t % 2) * 4 + 1:(t % 2) * 4 + 2]
            kss = ("ss", t % 2)
            P.op("dve", call(V.memset, ss, 0.0), W=[kss])
            P.op("act", call(A.activation, out=sqbuf[:], in_=xap, func=AF.Square, accum_out=ss),
                 R=[xkey, kss], W=["sqbuf", kss])
            P.op("act", call(A.activation, out=rs, in_=ss, func=AF.Ln, bias=epsT[:, 0:1], scale=1.0 / D),
                 R=[kss, "epsT"], W=[("rs", t % 2)])
            P.op("act", call(A.activation, out=rs, in_=rs, func=AF.Exp, scale=-0.5), R=[("rs", t % 2)], W=[("rs", t % 2)])
            P.op("dve", call(V.tensor_single_scalar, u[:], xap, rs, ALU.mult),
                 R=[xkey, ("rs", t % 2)], W=[ku])
            b = rb.next()
            pb = bank_bf(b)
            for kc in range(NKC):
                P.op("pe", call(T.transpose, pb[:, kc * 128:(kc + 1) * 128], u[:, kc * 128:(kc + 1) * 128], ident[:]),
                     R=[ku, "ident"], W=[("ps", b)])
            P.op("dve", call(V.tensor_copy, uT[:, :, t * 128:(t + 1) * 128],
                                                         pb.rearrange("p (k n) -> p k n", n=128)),
                 R=[("ps", b)], W=[(uT_key, t)])

    ogT = mem.alloc("ogT", [128, NKC, S], BF16)
    l0_after_og = mem.off
    uT0 = mem.alloc("uT0", [128, NKC, S], BF16)
    qg0 = mem.off
    QTa = [mem.alloc(f"QTa{i}", [128, S], BF16) for i in range(4)]
    gsb = [mem.alloc(f"gs{i}", [128, S], BF16) for i in range(2)]
    qg1 = mem.off
    KTa = [mem.alloc(f"KTa{i}", [128, S], BF16) for i in range(4)]
    Vp = [mem.alloc(f"Vp{i}", [128, NT, 192], BF16) for i in range(2)]
    c3t = mem.alloc("c3t", [128, S], BF16)
    c3 = [c3t[32 * i:32 * i + 16, :] for i in range(3)]
    negc_tok = mem.alloc("negc_tok", [128, NT, 16], F32)
    regx = mem.off
    mem.off = l0_base
    xs4 = [mem.alloc(f"xs4_{i}", [128, D], F32) for i in range(4)]
    ubuf = [mem.alloc(f"ubuf{i}", [128, D], BF16) for i in range(2)]
    sqbuf = mem.alloc("sqbuf", [128, D], F32)
    assert mem.off <= l0_after_og
    mem.off = l0_base
    lfA = mem.alloc("lfA", [128, S], F32)
    lfB = mem.alloc("lfB", [128, S], F32)
    onesS = mem.alloc("onesS", [128, S], F32)
    cmid0 = mem.alloc("cmid0", [128, S], BF16)
    assert mem.off <= l0_after_og
    mem.off = regx
    PT = [mem.alloc(f"PT{i}", [128, 512], BF16) for i in range(3)]
    NQF = 3
    qf = [mem.alloc(f"qf{i}", [128, 512], F32) for i in range(NQF)]
    sqb = [mem.alloc(f"sqb{i}", [128, 512], BF16) for i in range(NQF)]
    rsb = [mem.alloc(f"rsb{i}", [128, 512], F32) for i in range(NQF)]
    rinv = [mem.alloc(f"rinv{i}", [128, 512], F32) for i in range(2)]
    otmp = [mem.alloc(f"otmp{i}", [128, 512], F32) for i in range(2)]
    l0_end_d = mem.off

    events = []
    seq = [0]

    def at(time, fn):
        seq[0] += 1
        events.append((time, seq[0], fn))

    uT0_keys = [("uT0", t) for t in range(NT)]
    trb = Rot([0, 1, 2, 3])

    def a_tile(t, TA):
        st = {}
        xs = xs4[t % 4]
        kxs = ("xs4", t % 4)
        u = ubuf[t % 2]
        ku = ("u", t % 2)
        ss = small[:, (t % 4) * 4:(t % 4) * 4 + 1]
        rs = small[:, (t % 4) * 4 + 1:(t % 4) * 4 + 2]
        kss, krs = ("ss", t % 4), ("rs", t % 4)

        def ldx():
            P.dma(call(SP.dma_start, out=xs[:], in_=x_d[t * 128:(t + 1) * 128, :]), R=[], W=[kxs], sem=f"xs4_{t % 4}")

        def n1():
            P.op("dve", call(V.memset, ss, 0.0), W=[kss])
            P.op("act", call(A.activation, out=sqbuf[:], in_=xs[:], func=AF.Square, accum_out=ss), R=[kxs, kss], W=["sqbuf", kss])

        def n2():
            P.op("act", call(A.activation, out=rs, in_=ss, func=AF.Ln, bias=epsT[:, 0:1], scale=1.0 / D), R=[kss, "epsT"], W=[krs])

        def n3():
            P.op("act", call(A.activation, out=rs, in_=rs, func=AF.Exp, scale=-0.5), R=[krs], W=[krs])

        def n4():
            P.op("dve", call(V.tensor_single_scalar, u[:], xs[:], rs, ALU.mult), R=[kxs, krs], W=[ku])

        def n5():
            b = st["b"] = trb.next()
            pb = bank_bf(b)
            for kc in range(NKC):
                P.op("pe", call(T.transpose, pb[:, kc * 128:(kc + 1) * 128], u[:, kc * 128:(kc + 1) * 128], ident[:]),
                     R=[ku, "ident"], W=[("ps", b)])

        def n6():
            b = st["b"]
            P.op("dve", call(V.tensor_copy, uT0[:, :, t * 128:(t + 1) * 128], bank_bf(b).rearrange("p (k n) -> p k n", n=128)),
                 R=[("ps", b)], W=[("uT0", t)])
        at(TA - 7.9, ldx)
        for off, fn in ((0, n1), (1, n2), (2, n3), (4, n4), (6, n5), (8, n6)):
            at(TA + off, fn)

    for t in range(NT):
        a_tile(t, -220.0 + 3.5 * t)

    dmaH, dmaF, dmaO = {}, {}, {}

    def w_prologue():
        dmaH[0] = w_dma(wA_hp[0], 512)
        dmaF["h"] = w_dma(wA_f[:, :, :], 16)
    at(-232.0, w_prologue)

    holdf = {}

    def c_load():
        holdf["w"] = w_cast(dmaF["h"], PRM_GA)
        P.op("pool", call(G.memset, onesS[0:16, :], 1.0), R=uT0_keys, W=["onesS"])
        for i in range(4):
            P.op("pool", call(G.memset, KTa[i][64:67, :], 1.0), W=[("KTa_aug", i)])
        for i in range(2):
            P.op("pool", call(G.memset, Vp[i][:, :, 64:128], 1.0), W=[("Vp_ones", i)])
    at(-158.0, c_load)

    def c_f(tc):
        wbf, kwbf = holdf["w"]
        b = 3 + tc % 2
        for kc in range(NKC):
            P.op("pe", call(T.matmul, bank(b, 512, 0, 16), wbf[:, kc, 0:16], uT0[:, kc, tc * 512:(tc + 1) * 512],
                            start=(kc == 0), stop=(kc == NKC - 1)), R=[kwbf] + uT0_keys, W=[("ps", b)])
        P.op("act", call(A.activation, out=lfA[0:16, tc * 512:(tc + 1) * 512], in_=bank(b, 512, 0, 16), func=AF.Exp,
                         bias=negb[0:16, 0:1], scale=-1.0), R=[("ps", b), "negb"], W=[("lfA", tc)])
    for tc in range(4):
        at(-150.0 + 1.5 * tc, lambda tc=tc: c_f(tc))

    def c_ln():
        P.op("act", call(A.activation, out=lfA[0:16, :], in_=lfA[0:16, :], func=AF.Ln, bias=1.0, scale=1.0),
             R=[("lfA", tc) for tc in range(4)], W=["lfA_all"])
    at(-142.0, c_ln)

    def c_scan():
        P.op("dve", call(V.tensor_tensor_scan, lfB[0:16, :], onesS[0:16, :], lfA[0:16, :], 0.0, ALU.mult, ALU.add),
             R=["lfA_all", "onesS"], W=["lfB"])
    at(-138.0, c_scan)

    def c_tr(t):
        b = t % 2
        P.op("pe", call(T.transpose, bank(b, 16), lfB[0:16, t * 128:(t + 1) * 128], identf[0:16, 0:16]), R=["lfB", "identf"], W=[("ps", b)])
        P.op("dve", call(V.tensor_copy, negc_tok[:, t, :], bank(b, 16)), R=[("ps", b)], W=["negc_tok"])
    for t in range(NT):
        at(-132.0 + 0.5 * t, lambda t=t: c_tr(t))

    def c_split():
        P.op("dve", call(V.tensor_single_scalar, c3[0], lfB[0:16, :], -1.0, ALU.mult), R=["lfB"], W=["c3_0"])
        P.op("dve", call(V.scalar_tensor_tensor, lfA[0:16, :], lfB[0:16, :], -1.0, c3[0], ALU.mult, ALU.subtract),
             R=["lfB", "c3_0", "lfA_all"], W=["lfA_r1"])
        P.op("dve", call(V.tensor_copy, cmid0[0:16, :], lfA[0:16, :]), R=["lfA_r1"], W=["cmid0"])
        P.op("dve", call(V.tensor_copy, c3[1], cmid0[0:16, :]), R=["cmid0"], W=["c3_1"])
        P.op("dve", call(V.tensor_tensor, lfB[0:16, :], lfA[0:16, :], cmid0[0:16, :], ALU.subtract),
             R=["lfA_r1", "cmid0", "negc_tok"], W=["lfB_r2"])
        P.op("dve", call(V.tensor_copy, c3[2], lfB[0:16, :]), R=["lfB_r2"], W=["c3_2"])
    at(-122.0, c_split)

    s_banks = Rot([0, 1, 2])
    o_banks = Rot([3, 4])
    p_banks = Rot([5, 6, 7])
    pt_rot = Rot([0, 1, 2])
    sc_rot = Rot(list(range(NQF)))
    ev_rot = Rot([0, 1])

    def sched_proj(c, base):
        par = c % 2
        gs, vp = gsb[par], Vp[par]
        kgs, kvp = ("gs", par), ("Vp", par)
        hold = {}

        def wload():
            hold["wb"], hold["kwb"] = w_cast(dmaH[c], PRM_GA)
            if c + 1 < 8:
                dmaH[c + 1] = w_dma(wA_hp[c + 1], 512)
            if c == 7:
                dmaO[0] = w_dma(wA_o[:, :, 0:512], 512)
                dmaO[1] = w_dma(wA_o[:, :, 512:1024], 512)
        at(base - 0.5, wload)

        def v_block(t, T0):
            st = {}

            def a():
                b = st["b"] = p_banks.next()
                wb, kwb = hold["wb"], hold["kwb"]
                for kc in range(NKC):
                    P.op("pe", call(T.matmul, bank(b, 128), uT0[:, kc, t * 128:(t + 1) * 128], wb[:, kc, 384:512],
                                    start=(kc == 0), stop=(kc == NKC - 1)), R=[kwb, ("uT0", t)], W=[("ps", b)])

            def bfn():
                b = st["b"]
                P.op("dve", call(V.tensor_copy, vp[:, t, :].rearrange("p (a n) -> p a n", n=64)[:, 0:3:2, :],
                                 bank(b, 128).rearrange("p (a n) -> p a n", n=64)),
                     R=[("ps", b), ("Vp_ones", par)], W=[kvp])
            at(T0, a)
            at(T0 + 1, bfn)

        def g_block(tc, T0):
            st = {}

            def a():
                b = st["b"] = p_banks.next()
                wb, kwb = hold["wb"], hold["kwb"]
                for kc in range(NKC):
                    P.op("pe", call(T.matmul, bank(b), wb[:, kc, 256:384], uT0[:, kc, tc * 512:(tc + 1) * 512],
                                    start=(kc == 0), stop=(kc == NKC - 1)), R=[kwb] + uT0_keys[tc * 4:tc * 4 + 4], W=[("ps", b)])

            def bfn():
                b = st["b"]
                P.op("dve", call(V.tensor_copy, gs[:, tc * 512:(tc + 1) * 512], bank(b)), R=[("ps", b)], W=[("gspre", par, tc), kgs])
            at(T0, a)
            at(T0 + 3, bfn)

        def qk_block(which, tc, T0):
            st = {}
            dst = QTa if which == 0 else KTa
            gcol = gq0s if which == 0 else prm[:, PRM_GK0:PRM_GK0 + 1]
            gkey = "gq0s" if which == 0 else "prm"

            def a():
                b = st["b"] = p_banks.next()
                wb, kwb = hold["wb"], hold["kwb"]
                for kc in range(NKC):
                    P.op("pe", call(T.matmul, bank(b), wb[:, kc, which * 128:(which + 1) * 128], uT0[:, kc, tc * 512:(tc + 1) * 512],
                                    start=(kc == 0), stop=(kc == NKC - 1)), R=[kwb] + uT0_keys[tc * 4:tc * 4 + 4], W=[("ps", b)])

            def a2():
                si = st["si"] = sc_rot.next()
                P.op("dve", call(V.tensor_copy, qf[si][:], bank(st["b"])), R=[("ps", st["b"])], W=[("qf", si)])
                P.op("pool", call(G.tensor_tensor, sqb[si][:], qf[si][:], qf[si][:], ALU.mult), R=[("qf", si)], W=[("sqb", si)])

            def cfn():
                b2 = st["b2"] = p_banks.next()
                P.op("pe", call(T.matmul, bank(b2), bd[:], sqb[st["si"]][:], start=True, stop=True), R=[("sqb", st["si"]), "bd"], W=[("ps", b2)])

            def d1():
                si = st["si"]
                P.op("act", call(A.activation, out=rsb[si][:], in_=bank(st["b2"]), func=AF.Ln, bias=epsT[:, 0:1]),
                     R=[("ps", st["b2"]), "epsT"], W=[("rsb", si)])

            def d2():
                si = st["si"]
                P.op("act", call(A.activation, out=rsb[si][:], in_=rsb[si][:], func=AF.Exp, scale=-0.5), R=[("rsb", si)], W=[("rsb", si)])

            def e():
                si = st["si"]
                for hh in range(2):
                    pp = slice(hh * 64, hh * 64 + 64)
                    P.op("dve", call(V.scalar_tensor_tensor, dst[2 * par + hh][0:64, tc * 512:(tc + 1) * 512], qf[si][pp, :],
                                     gcol[pp, 0:1], rsb[si][pp, :], ALU.mult, ALU.mult),
                         R=[("qf", si), ("rsb", si), gkey], W=[("QK", which, 2 * par + hh, tc)])
            for off, fn in ((0, a), (3, a2), (9, cfn), (11, d1), (13, d2), (15, e)):
                at(T0 + off, fn)

        blocks = [("g", tc) for tc in range(4)]
        vt = 0
        for tc in range(4):
            for which in range(2):
                blocks.append(("qk", which, tc))
                for _ in range(2):
                    blocks.append(("v", vt))
                    vt += 1
        for k, blk in enumerate(blocks):
            T0 = base + 1 + k * 2.5
            if blk[0] == "g":
                g_block(blk[1], T0)
            elif blk[0] == "v":
                v_block(blk[1], T0)
            else:
                qk_block(blk[1], blk[2], T0)

        def silu_all():
            P.op("act", call(A.activation, out=gs[:, :], in_=gs[:, :], func=AF.Silu), R=[("gspre", par, tc) for tc in range(4)], W=[kgs])
        at(base + 84.0, silu_all)

        def aug():
            for hh in range(2):
                h = 2 * c + hh
                for r in range(3):
                    P.dma(call(SP.dma_start, out=QTa[2 * par + hh][64 + r:65 + r, :], in_=c3t[32 * r + h:32 * r + h + 1, :]),
                          R=[f"c3_{r}"], W=[("Qaug", 2 * par + hh, r)], sem=f"qaug{2 * par + hh}_{r}")
        at(max(base + 0.2, -121.0), aug)

    def sched_attn(c, base):
        par = c % 2
        gs, vp = gsb[par], Vp[par]
        kgs, kvp = ("gs", par), ("Vp", par)
        tiles = [(hh, qc, j) for hh in HH_ORDER for qc in range(4) for j in range(4 * qc + 4)]
        info = {}
        obank = {}
        fin = {}
        LOOK = 2

        def emit_S(i):
            hh, qc, j = tiles[i]
            h = 2 * c + hh
            sl = 2 * par + hh
            qt, kt = QTa[sl], KTa[sl]
            qs = max(qc * 512, j * 128)
            n = (qc + 1) * 512 - qs
            sb = s_banks.next()
            diag = j >= 4 * qc
            qk_keys = [("QK", 0, sl, qc), ("Qaug", sl, 0), ("Qaug", sl, 1), ("Qaug", sl, 2), ("QK", 1, sl, j // 4), ("KTa_aug", sl)]
            P.op("pe", call(T.matmul, bank(sb, n), kt[0:67, j * 128:(j + 1) * 128], qt[0:67, qs:qs + n], start=True, stop=(not diag)),
                 R=qk_keys, W=[("ps", sb)])
            if diag:
                P.op("pe", call(T.matmul, bank(sb, 128), ident[:], maskC[:], start=False, stop=True), R=["ident", "maskC"], W=[("ps", sb)])
            pi = pt_rot.next()
            P.op("act", call(A.activation, out=PT[pi][:, 0:n], in_=bank(sb, n), func=AF.Exp, bias=negc_tok[:, j, h:h + 1], scale=1.0),
                 R=[("ps", sb), "negc_tok"], W=[("PT", pi)])
            info[i] = (pi, n, qs - qc * 512)

        def emit_PV(i):
            hh, qc, j = tiles[i]
            pi, n, coff = info[i]
            vcol = 0 if hh == 0 else 64
            o0, s0 = (0, 64) if hh == 0 else (64, 0)
            nj = 4 * qc + 4
            if j == 0:
                obank[(hh, qc)] = o_banks.next()
            ob = obank[(hh, qc)]
            P.op("pe", call(T.matmul, bank(ob, n, c0=coff), vp[:, j, vcol:vcol + 128], PT[pi][:, 0:n], start=(j == 0), stop=(j == nj - 1)),
                 R=[("PT", pi), kvp, ("Vp_ones", par)], W=[("ps", ob)])
            if j == nj - 1:
                ei = ev_rot.next()
                P.op("act", call(A.activation, out=rinv[ei][s0:s0 + 64, :], in_=bank(ob, 512, s0, s0 + 64), func=AF.Ln), R=[("ps", ob)], W=[("rinv", ei)])
                P.op("act", call(A.activation, out=rinv[ei][o0:o0 + 64, :], in_=rinv[ei][s0:s0 + 64, :], func=AF.Exp, scale=-1.0),
                     R=[("rinv", ei)], W=[("rinv", ei)])
                fin[(hh, qc)] = (ei, ob, o0)

        def emit_fin(hh, qc):
            ei, ob, o0 = fin[(hh, qc)]
            P.op("dve", call(V.tensor_tensor, otmp[ei][o0:o0 + 64, :], bank(ob, 512, o0, o0 + 64), rinv[ei][o0:o0 + 64, :], ALU.mult),
                 R=[("ps", ob), ("rinv", ei)], W=[("otmp", ei)])
            P.op("pool", call(G.tensor_tensor, ogT[o0:o0 + 64, c, qc * 512:(qc + 1) * 512], otmp[ei][o0:o0 + 64, :],
                              gs[o0:o0 + 64, qc * 512:(qc + 1) * 512], ALU.mult),
                 R=[("otmp", ei), kgs], W=[("ogT", c, qc)])

        for i in range(len(tiles)):
            at(base + i, lambda i=i: emit_S(i))
            at(base + i + LOOK + 0.1, lambda i=i: emit_PV(i))
            hh_, qc_, j_ = tiles[i]
            if j_ == 4 * qc_ + 3:
                at(base + i + LOOK + 3.6, lambda hh_=hh_, qc_=qc_: emit_fin(hh_, qc_))

    sched_proj(0, -160.0)
    for c in range(8):
        sched_attn(c, c * 100.0)
        if c + 1 < 8:
            sched_proj(c + 1, c * 100.0)
    events.sort(key=lambda e: (e[0], e[1]))
    for _, _, fn in events:
        fn()

    P.barrier()

    mem.off = l0_after_og
    h1 = mem.alloc("h1", [128, NT, D], F32)
    l1_base = mem.off
    mem.off = l0_base
    uT1 = mem.alloc("uT1", [128, NKC, S], BF16)
    assert mem.off <= l0_after_og
    mem.off = l1_base
    KTs1 = mem.alloc("KTs1", [128, S], BF16)
    V1g = mem.alloc("V1g", [128, NT, 192], BF16)
    Cfm = mem.alloc("Cfm", [128, S], F32)
    Sfm = mem.alloc("Sfm", [128, S], F32)
    grp0 = mem.off
    QT1 = mem.alloc("QT1", [128, 2, S], BF16)
    gs1 = mem.alloc("gs1", [128, 2, S], BF16)
    og1 = mem.alloc("og1", [128, 2, S], BF16)
    mem.off = grp0
    posb = mem.alloc("posb", [128, S], I32)
    angA = mem.alloc("angA", [128, S], F32)
    angB = mem.alloc("angB", [128, S], F32)
    mem.off = l1_base
    ubuf1 = [mem.alloc(f"ubuf1_{i}", [128, D], BF16) for i in range(2)]
    sq1 = mem.alloc("sq1", [128, D], F32)
    assert mem.off <= l1_base + 4096 + 6144

    ev2 = []
    sq2 = [0]

    def at2(time, fn):
        sq2[0] += 1
        ev2.append((time, sq2[0], fn))

    hold_o = {}

    def ld_wo():
        hold_o[0] = w_cast(dmaO[0], None)
        hold_o[1] = w_cast(dmaO[1], None)
    at2(-20, ld_wo)
    eb = Rot([0, 1, 2, 3, 4, 5])
    tb2 = Rot([6, 7])
    uT1_keys = [("uT1", t) for t in range(NT)]

    def e_tile(t):
        st = {}
        xs = XS[t % 2]
        kxs = ("xs", t % 2)

        def ldx():
            P.dma(call(SP.dma_start, out=xs[:], in_=x_d[t * 128:(t + 1) * 128, :]), R=[], W=[kxs], sem=f"xs{t % 2}")

        def mm():
            st["b"] = [eb.next(), eb.next()]
            for hf in range(2):
                w, kw = hold_o[hf]
                for kc in range(NKC):
                    P.op("pe", call(T.matmul, bank(st["b"][hf]), ogT[:, kc, t * 128:(t + 1) * 128], w[:, kc, :],
                                    start=(kc == 0), stop=(kc == NKC - 1)), R=[kw, ("ogt", t)], W=[("ps", st["b"][hf])])

        def add():
            for hf in range(2):
                P.op("dve", call(V.tensor_tensor, h1[:, t, hf * 512:(hf + 1) * 512], bank(st["b"][hf]), xs[:, hf * 512:(hf + 1) * 512], ALU.add),
                     R=[("ps", st["b"][hf]), kxs], W=[("h1", t, hf)])

        u = ubuf1[t % 2]
        ku = ("u1", t % 2)
        ss = small[:, (t % 2) * 4:(t % 2) * 4 + 1]
        rs = small[:, (t % 2) * 4 + 1:(t % 2) * 4 + 2]
        kss, krs = ("ss", t % 2), ("rs", t % 2)

        def n1():
            P.op("dve", call(V.memset, ss, 0.0), W=[kss])
            P.op("act", call(A.activation, out=sq1[:], in_=h1[:, t, :], func=AF.Square, accum_out=ss),
                 R=[("h1", t, 0), ("h1", t, 1), kss], W=["sq1", kss])

        def n2():
            P.op("act", call(A.activation, out=rs, in_=ss, func=AF.Ln, bias=epsT[:, 0:1], scale=1.0 / D), R=[kss, "epsT"], W=[krs])

        def n3():
            P.op("act", call(A.activation, out=rs, in_=rs, func=AF.Exp, scale=-0.5), R=[krs], W=[krs])

        def n4():
            P.op("dve", call(V.tensor_single_scalar, u[:], h1[:, t, :], rs, ALU.mult), R=[("h1", t, 0), ("h1", t, 1), krs], W=[ku])

        def n5():
            b = st["tb"] = tb2.next()
            pb = bank_bf(b)
            for kc in range(NKC):
                P.op("pe", call(T.transpose, pb[:, kc * 128:(kc + 1) * 128], u[:, kc * 128:(kc + 1) * 128], ident[:]),
                     R=[ku, "ident"], W=[("ps", b)])

        def n6():
            b = st["tb"]
            P.op("dve", call(V.tensor_copy, uT1[:, :, t * 128:(t + 1) * 128], bank_bf(b).rearrange("p (k n) -> p k n", n=128)),
                 R=[("ps", b)], W=[("uT1", t), ("ogt", t)])
        T0 = 5.0 * t
        at2(T0 - 10 + 3.5, ldx)
        at2(T0, mm)
        at2(T0 + 3, add)
        at2(T0 + 5, n1)
        at2(T0 + 7, n2)
        at2(T0 + 8, n3)
        at2(T0 + 9, n4)
        at2(T0 + 11, n5)
        at2(T0 + 14, n6)

    for t in range(NT):
        e_tile(t)

    tbl = []
    tbl.append(lambda: P.dma(call(G.dma_start, out=angA[:], in_=posb_d[:, :]), R=[], W=["angA"], sem="posb", queue="pool"))
    tbl.append(lambda: P.op("dve", call(V.tensor_single_scalar, angA[:], angA[:], prm[:, PRM_INVF:PRM_INVF + 1], ALU.mult), R=["angA", "prm"], W=["angA"]))
    tbl.append(lambda: P.op("dve", call(V.tensor_single_scalar, angB[:], angA[:], PI / 2, ALU.add), R=["angA"], W=["angB"]))
    kint = posb[:]
    for nm, ang, kflt, dstT in (("c", angB, Sfm, Cfm), ("s", angA, angB, Sfm)):
        ka = "ang" + ("B" if nm == "c" else "A")
        kk = "kf" + nm
        tbl.append(lambda ang=ang, kflt=kflt, ka=ka, kk=kk: P.op("dve", call(V.tensor_single_scalar, kflt[:], ang[:], 1.0 / TWO_PI, ALU.mult), R=[ka, "Cfm_done"], W=[kk]))
        tbl.append(lambda kflt=kflt, kk=kk: P.op("dve", call(V.tensor_copy, kint, kflt[:]), R=[kk], W=["kint"]))
        tbl.append(lambda kflt=kflt, kk=kk: P.op("dve", call(V.tensor_copy, kflt[:], kint), R=["kint"], W=[kk]))
        tbl.append(lambda ang=ang, kflt=kflt, ka=ka, kk=kk: P.op("dve", call(V.scalar_tensor_tensor, ang[:], kflt[:], -TWO_PI, ang[:], ALU.mult, ALU.add), R=[kk, ka], W=[ka]))
        tbl.append(lambda ang=ang, kflt=kflt, ka=ka, kk=kk: P.op("dve", call(V.tensor_scalar, kflt[:], ang[:], PI, -TWO_PI, ALU.is_gt, ALU.mult), R=[ka], W=[kk]))
        tbl.append(lambda ang=ang, kflt=kflt, ka=ka, kk=kk: P.op("dve", call(V.tensor_tensor, ang[:], ang[:], kflt[:], ALU.add), R=[kk, ka], W=[ka]))
        tbl.append(lambda ang=ang, kflt=kflt, ka=ka, kk=kk: P.op("dve", call(V.tensor_scalar, kflt[:], ang[:], -PI, TWO_PI, ALU.is_lt, ALU.mult), R=[ka], W=[kk]))
        tbl.append(lambda ang=ang, kflt=kflt, ka=ka, kk=kk: P.op("dve", call(V.tensor_tensor, ang[:], ang[:], kflt[:], ALU.add), R=[kk, ka], W=[ka]))
        tbl.append(lambda ang=ang, dstT=dstT, ka=ka, nm=nm: P.op("act", call(A.activation, out=dstT[:], in_=ang[:], func=AF.Sin), R=[ka], W=["Cfm_done" if nm == "c" else "Sfm_raw"]))
    tbl.append(lambda: P.op("dve", call(V.tensor_single_scalar, Sfm[:], Sfm[:], prm[:, PRM_SIGN:PRM_SIGN + 1], ALU.mult), R=["Sfm_raw", "prm"], W=["Sfm"]))
    for k, fn in enumerate(tbl):
        at2(-15 + 4.2 * k, fn)

    ev2.sort(key=lambda e: (e[0], e[1]))
    for _, _, fn in ev2:
        fn()
    P.barrier()

    mem.off = wreg0
    WS1 = [mem.alloc(f"WS1_{i}", [128, NKC, 256], F32) for i in range(2)]
    WB1 = [mem.alloc(f"WB1_{i}", [128, NKC, 256], BF16) for i in range(3)]
    permf = mem.alloc("permf", [128, 128], F32)
    permb = mem.alloc("permb", [128, 128], BF16)
    PT1 = [mem.alloc(f"PT1_{i}", [128, 1024], BF16) for i in range(2)]
    NQ1 = 3
    qf1 = [mem.alloc(f"qf1_{i}", [128, 512], F32) for i in range(NQ1)]
    sqb1 = [mem.alloc(f"sqb1_{i}", [128, 512], BF16) for i in range(2)]
    rsb1 = [mem.alloc(f"rsb1_{i}", [128, 512], F32) for i in range(NQ1)]
    qn1 = [mem.alloc(f"qn1_{i}", [128, 512], BF16) for i in range(2)]
    den1 = [mem.alloc(f"den1_{i}", [128, 512], F32) for i in range(2)]
    otmp1 = [mem.alloc(f"otmp1_{i}", [128, 256], F32) for i in range(2)]
    assert mem.off <= wreg1, (mem.off, wreg1)
    wrot["ws"] = 0
    wrot["wb"] = 0
    P.op("pool", call(G.memset, mtmp[:], 0.0), R=["mtmp"], W=["mtmp"])
    P.op("pool", call(G.affine_select, out=mtmp[:], in_=mtmp[:], compare_op=ALU.not_equal, fill=1.0,
                      base=-8, pattern=[[1, 128]], channel_multiplier=-1), R=["mtmp"], W=["mtmp"])
    P.op("dve", call(V.tensor_single_scalar, permf[:], mtmp[:], prm[:, PRM_M1:PRM_M1 + 1], ALU.mult), R=["mtmp", "prm"], W=["permf"])
    P.op("pool", call(G.memset, mtmp[:], 0.0), R=["mtmp"], W=["mtmp"])
    P.op("pool", call(G.affine_select, out=mtmp[:], in_=mtmp[:], compare_op=ALU.not_equal, fill=1.0,
                      base=8, pattern=[[1, 128]], channel_multiplier=-1), R=["mtmp"], W=["mtmp"])
    P.op("dve", call(V.scalar_tensor_tensor, permf[:], mtmp[:], prm[:, PRM_M2:PRM_M2 + 1], permf[:], ALU.mult, ALU.add),
         R=["mtmp", "prm", "permf"], W=["permf"])
    P.op("dve", call(V.tensor_copy, permb[:], permf[:]), R=["permf"], W=["permb"])
    P.op("act", call(A.activation, out=es_bc[:], in_=prm[:, PRM_SINK:PRM_SINK + 16], func=AF.Exp), R=["prm"], W=["es_bc"])
    P.op("dve", call(V.tensor_single_scalar, gq1s[:], prm[:, PRM_GQ1:PRM_GQ1 + 1], 0.125, ALU.mult), R=["prm"], W=["gq1s"])

    P.op("pool", call(G.memset, V1g[:, :, 0:64], 1.0), W=["V1_ones"])
    P.op("pool", call(G.memset, V1g[:, :, 128:192], 1.0), W=["V1_ones"])
    pjb = Rot([4, 5, 6, 7, 0, 1, 2, 3])
    s1b = Rot([0, 2])
    o1b = Rot([4, 5])
    pt1 = Rot([0, 1])
    ev1 = Rot([0, 1])
    qf_rot = Rot(list(range(NQ1)))
    sq_rot = Rot([0, 1])
    qn_rot = Rot([0, 1])
    events1 = []
    seq1 = [0]

    def at1(time, fn):
        seq1[0] += 1
        events1.append((time, seq1[0], fn))

    def sched_group(g, base):
        hold = {}

        def wsrc(kind, gg):
            if kind == "kv":
                return wB_kv[gg], 192
            if kind == "q":
                return wB_q[:, :, gg * 256:(gg + 1) * 256], 256
            if kind == "g":
                return wB_g[:, :, gg * 256:(gg + 1) * 256], 256
            return wB_o[:, 2 * gg:2 * gg + 2, :].rearrange("p k (a n) -> p (k a) n", n=256), 256

        def dma1(kind, gg):
            src, nc_ = wsrc(kind, gg)
            dma1h[(kind, gg)] = w_dma(src, nc_, ws_list=WS1)

        def start():
            if g == 0:
                dma1("kv", 0)
                dma1("q", 0)
            hold["kv"] = w_cast(dma1h[("kv", g)], PRM_GKV, wb_list=WB1)
            hold["q"] = w_cast(dma1h[("q", g)], PRM_GB, wb_list=WB1)
            dma1("g", g)
            dma1("o", g)

        def ld_g():
            hold["g"] = w_cast(dma1h[("g", g)], PRM_GB, wb_list=WB1)
            if g + 1 < 4:
                dma1("kv", g + 1)

        def ld_o():
            hold["o"] = w_cast(dma1h[("o", g)], None, wb_list=WB1)
            if g + 1 < 4:
                dma1("q", g + 1)
        at1(base - 2, start)
        at1(base + 30, ld_g)
        at1(base + 130, ld_o)

        def rope_block(T0, wname, col0, tc, gcol, gkey, dst, dkey):
            st = {}
            tok = slice(tc * 512, (tc + 1) * 512)

            def a():
                b = st["b"] = pjb.next()
                w, kw = hold[wname]
                for kc in range(NKC):
                    P.op("pe", call(T.matmul, bank(b), w[:, kc, col0:col0 + 128], uT1[:, kc, tok], start=(kc == 0), stop=(kc == NKC - 1)),
                         R=[kw] + uT1_keys[tc * 4:tc * 4 + 4], W=[("ps", b)])

            def a2():
                si = st["si"] = qf_rot.next()
                sq = st["sq"] = sq_rot.next()
                P.op("act", call(A.activation, out=qf1[si][:], in_=bank(st["b"]), func=AF.Copy), R=[("ps", st["b"])], W=[("qf1", si)])
                P.op("pool", call(G.tensor_tensor, sqb1[sq][:], qf1[si][:], qf1[si][:], ALU.mult), R=[("qf1", si)], W=[("sqb1", sq)])

            def c_():
                b2 = st["b2"] = pjb.next()
                P.op("pe", call(T.matmul, bank(b2), bd[:], sqb1[st["sq"]][:], start=True, stop=True), R=[("sqb1", st["sq"]), "bd"], W=[("ps", b2)])

            def d1():
                si = st["si"]
                P.op("act", call(A.activation, out=rsb1[si][:], in_=bank(st["b2"]), func=AF.Ln, bias=epsT[:, 0:1]),
                     R=[("ps", st["b2"]), "epsT"], W=[("rsb1", si)])

            def d2():
                si = st["si"]
                P.op("act", call(A.activation, out=rsb1[si][:], in_=rsb1[si][:], func=AF.Exp, scale=-0.5), R=[("rsb1", si)], W=[("rsb1", si)])

            def e():
                si = st["si"]
                qi = st["qi"] = qn_rot.next()
                P.op("dve", call(V.scalar_tensor_tensor, qn1[qi][:], qf1[si][:], gcol, rsb1[si][:], ALU.mult, ALU.mult),
                     R=[("qf1", si), ("rsb1", si), gkey], W=[("qn1", qi)])

            def f():
                b3 = st["b3"] = pjb.next()
                P.op("pe", call(T.matmul, bank(b3), permb[:], qn1[st["qi"]][:], start=True, stop=True), R=[("qn1", st["qi"]), "permb"], W=[("ps", b3)])
                si, qi = st["si"], st["qi"]
                P.op("pool", call(G.tensor_tensor, qf1[si][:], qn1[qi][:], Cfm[:, tok], ALU.mult), R=[("qn1", qi), "Cfm_done", ("qf1", si)], W=[("qf1", si)])

            def g_():
                si, qi, b3 = st["si"], st["qi"], st["b3"]
                P.op("dve", call(V.tensor_tensor, bank(b3), bank(b3), Sfm[:, tok], ALU.mult), R=[("ps", b3), "Sfm"], W=[("ps", b3)])
                P.op("dve", call(V.tensor_tensor, dst, bank(b3), qf1[si][:], ALU.add), R=[("ps", b3), ("qf1", si)], W=[dkey])
            for off, fn in ((0, a), (3, a2), (9, c_), (11, d1), (13, d2), (15, e), (19, f), (23, g_)):
                at1(T0 + off, fn)

        def gate_block(T0, ch, tc):
            st = {}

            def a():
                b = st["b"] = pjb.next()
                w, kw = hold["g"]
                for kc in range(NKC):
                    P.op("pe", call(T.matmul, bank(b), w[:, kc, ch * 128:(ch + 1) * 128], uT1[:, kc, tc * 512:(tc + 1) * 512],
                                    start=(kc == 0), stop=(kc == NKC - 1)), R=[kw] + uT1_keys[tc * 4:tc * 4 + 4], W=[("ps", b)])

            def a2():
                P.op("act", call(A.activation, out=gs1[:, ch, tc * 512:(tc + 1) * 512], in_=bank(st["b"]), func=AF.Copy), R=[("ps", st["b"])], W=[("gs1pre", ch, tc), ("gs1", ch)])
            at1(T0, a)
            at1(T0 + 3, a2)

        def v_block(T0, t):
            st = {}

            def a():
                b = st["b"] = pjb.next()
                w, kw = hold["kv"]
                for kc in range(NKC):
                    P.op("pe", call(T.matmul, bank(b, 64), uT1[:, kc, t * 128:(t + 1) * 128], w[:, kc, 128:192],
                                    start=(kc == 0), stop=(kc == NKC - 1)), R=[kw, ("uT1", t)], W=[("ps", b)])

            def a2():
                P.op("dve", call(V.tensor_copy, V1g[:, t, 64:128], bank(st["b"], 64)), R=[("ps", st["b"]), "V1_ones"], W=[("V1g", t)])
            at1(T0, a)
            at1(T0 + 2, a2)

        blocks = []
        vi_ = 0
        for tc in range(4):
            blocks.append(("k", tc))
            for _ in range(2):
                blocks.append(("v", vi_)); vi_ += 1
        for r in range(8):
            blocks.append(("q", r // 4, r % 4))
            blocks.append(("g", r // 4, r % 4))
            blocks.append(("v", vi_)); vi_ += 1
        for k, blk in enumerate(blocks):
            T0 = base + 1 + 3.6 * k
            if blk[0] == "k":
                rope_block(T0, "kv", 0, blk[1], prm[:, PRM_GK1:PRM_GK1 + 1], "prm", KTs1[:, blk[1] * 512:(blk[1] + 1) * 512], ("KTs", blk[1]))
            elif blk[0] == "q":
                ch, tc = blk[1], blk[2]
                rope_block(T0, "q", ch * 128, tc, gq1s[:, 0:1], "gq1s", QT1[:, ch, tc * 512:(tc + 1) * 512], ("QT1", ch, tc))
            elif blk[0] == "g":
                gate_block(T0, blk[1], blk[2])
            else:
                v_block(T0, blk[1])
        tproj = base + 1 + 3.6 * len(blocks)

        def silu_all():
            for ch in range(2):
                P.op("act", call(A.activation, out=gs1[:, ch, :], in_=gs1[:, ch, :], func=AF.Silu),
                     R=[("gs1pre", ch, tc) for tc in range(4)], W=[("gs1", ch)])
        at1(tproj + 8, silu_all)

        ob_of = {}
        sinfo = {}
        fin_info = {}

        def emit_S1(j):
            nq = 256 if j < NT - 1 else 128
            sb0 = s1b.next()
            qkeys = sorted(set([("QT1", ch, (j * 128) // 512) for ch in range(2)] + [("QT1", ch, (j * 128 + nq - 1) // 512) for ch in range(2)]))
            for ch in range(2):
                co = ch * 256
                for half in range(2):
                    p0 = half * 64
                    sbk = sb0 + half
                    P.op("pe", call(T.matmul, bank(sbk, nq, c0=co), KTs1[p0:p0 + 64, j * 128:(j + 1) * 128], QT1[p0:p0 + 64, ch, j * 128:j * 128 + nq],
                                    start=(ch == 0), stop=False, skip_group_check=True),
                         R=[("KTs", j // 4)] + qkeys, W=[("ps", sbk)])
                for half in range(2):
                    sbk = sb0 + half
                    P.op("pe", call(T.matmul, bank(sbk, 128, c0=co), ident[:], maskC[:], start=False, stop=(nq == 128), skip_group_check=True),
                         R=["ident", "maskC"], W=[("ps", sbk)])
                    if nq == 256:
                        P.op("pe", call(T.matmul, bank(sbk, 128, c0=co + 128), ident[:], maskW[:], start=False, stop=True, skip_group_check=True),
                             R=["ident", "maskW"], W=[("ps", sbk)])
            pi = pt1.next()
            if nq == 256:
                P.op("act", call(A.activation, out=PT1[pi][:], in_=psum[:, sb0 * 512:sb0 * 512 + 1024], func=AF.Exp),
                     R=[("ps", sb0), ("ps", sb0 + 1)], W=[("PT1", pi)])
            else:
                for half in range(2):
                    for ch in range(2):
                        pos = half * 2 + ch
                        P.op("act", call(A.activation, out=PT1[pi][:, pos * 256:pos * 256 + 128], in_=bank(sb0 + half, 128, c0=ch * 256), func=AF.Exp),
                             R=[("ps", sb0), ("ps", sb0 + 1)], W=[("PT1", pi)])
            sinfo[j] = pi

        def emit_PV1(j):
            pi = sinfo[j]
            pt4 = PT1[pi][:, :].rearrange("p (h q) -> p h q", q=256)
            for qi, qoff in ((j, 0), (j + 1, 128)):
                if qi >= NT:
                    continue
                first = (qi == j + 1) or (j == 0)
                last = (qi == j)
                if first:
                    ob_of[qi] = o1b.next()
                ob = ob_of[qi]
                for par in range(2):
                    lc = 64 if par == 0 else 0
                    P.op("pe", call(T.matmul, bank(ob, 256, c0=par * 256), V1g[:, j, lc:lc + 128], pt4[:, 2 * par:2 * par + 2, qoff:qoff + 128],
                                    start=(first and par == 0), stop=(last and par == 1), skip_group_check=True),
                         R=[("PT1", pi), ("V1g", j), "V1_ones"], W=[("ps", ob)])
                if not last:
                    continue
                ei = ev1.next()
                for par in range(2):
                    o0, s0 = (0, 64) if par == 0 else (64, 0)
                    P.op("dve", call(V.tensor_tensor,
                                     den1[ei][s0:s0 + 64, 0:256].rearrange("p (h q) -> p h q", q=128),
                                     bank(ob, 256, s0, s0 + 64, c0=par * 256).rearrange("p (h q) -> p h q", q=128),
                                     es_bc[s0:s0 + 64, 4 * g + par:4 * g + 4:2].unsqueeze(2).to_broadcast([64, 2, 128]), ALU.add),
                         R=[("ps", ob), "es_bc"], W=[("den1", ei, par)])
                P.op("act", call(A.activation, out=den1[ei][:, 0:256], in_=den1[ei][:, 0:256], func=AF.Ln),
                     R=[("den1", ei, 0), ("den1", ei, 1)], W=[("den1", ei, 0), ("den1", ei, 1)])
                for par in range(2):
                    o0, s0 = (0, 64) if par == 0 else (64, 0)
                    P.op("act", call(A.activation, out=den1[ei][o0:o0 + 64, 256:512], in_=den1[ei][s0:s0 + 64, 0:256], func=AF.Exp, scale=-1.0),
                         R=[("den1", ei, par)], W=[("den1b", ei, par)])
                fin_info[qi] = (ei, ob)

        def emit_fin1(qi):
            ei, ob = fin_info[qi]
            for par in range(2):
                o0, s0 = (0, 64) if par == 0 else (64, 0)
                P.op("dve", call(V.tensor_tensor, otmp1[ei][o0:o0 + 64, 0:256], bank(ob, 256, o0, o0 + 64, c0=par * 256),
                                 den1[ei][o0:o0 + 64, 256:512], ALU.mult),
                     R=[("ps", ob), ("den1b", ei, par)], W=[("otmp1", ei, par)])
            for par in range(2):
                o0, s0 = (0, 64) if par == 0 else (64, 0)
                P.op("pool", call(G.tensor_tensor, og1[o0:o0 + 64, :, qi * 128:(qi + 1) * 128],
                                  otmp1[ei][o0:o0 + 64, 0:256].rearrange("p (h q) -> p h q", q=128),
                                  gs1[o0:o0 + 64, :, qi * 128:(qi + 1) * 128], ALU.mult),
                     R=[("otmp1", ei, par), ("gs1", 0), ("gs1", 1)], W=[("og1", qi, par)])

        def emit_oproj(t):
            w, kw = hold["o"]
            for hf in range(2):
                b = 6 + hf
                for k in range(2):
                    P.op("pe", call(T.matmul, bank(b), og1[:, k, t * 128:(t + 1) * 128],
                                    w[:, k * 4 + 2 * hf:k * 4 + 2 * hf + 2, :].rearrange("p a n -> p (a n)"), start=(k == 0), stop=(k == 1)),
                         R=[kw, ("og1", t, 0), ("og1", t, 1)], W=[("ps", b)])
                P.op("dve", call(V.tensor_tensor, h1[:, t, hf * 512:(hf + 1) * 512], bank(b), h1[:, t, hf * 512:(hf + 1) * 512], ALU.add),
                     R=[("ps", b), ("h1", t, hf)], W=[("h1", t, hf)])

        ta = tproj + 14
        for j in range(NT):
            at1(ta + 6 * j, lambda j=j: emit_S1(j))
            at1(ta + 6 * (j + 1) + 0.1, lambda j=j: emit_PV1(j))
            at1(ta + 6 * (j + 1) + 5, lambda j=j: emit_fin1(j))
            at1(ta + 6 * (j + 4) + 0.05, lambda j=j: emit_oproj(j))
        return ta + 6 * (NT + 4) + 6

    tb = 0.0
    dma1h = {}
    for g in range(4):
        tb = sched_group(g, tb) + 4
    events1.sort(key=lambda e: (e[0], e[1]))
    for _, _, fn in events1:
        fn()

    for t in range(NT):
        P.dma(call(SP.dma_start, out=out_d[t * 128:(t + 1) * 128, :], in_=h1[:, t, :]), R=[("h1", t, 0), ("h1", t, 1)], W=[("out", t)], sem="outst")
    P.barrier()
    P.emit()
    return nc


def _pk(w):
    n = w.shape[1]
    return np.ascontiguousarray(w.reshape(NKC, 128, n).transpose(1, 0, 2))


def _prep_shared(positions, norm_a_g, w_in_a, b_forget, qnorm_a_g, knorm_a_g, w_out_a, kv_norm_g, w_kv, knorm_b_g,
                 norm_b_g, w_in_b, qnorm_b_g, sinks, w_out_b):
    f32 = np.float32
    wa = np.asarray(w_in_a, f32)[0]
    hp = []
    for c in range(8):
        cols = np.concatenate([np.arange(c * 128, c * 128 + 128), 1024 + np.arange(c * 128, c * 128 + 128),
                               3088 + np.arange(c * 128, c * 128 + 128), 2048 + np.arange(c * 128, c * 128 + 128)])
        hp.append(_pk(wa[:, cols]))
    wkv = np.asarray(w_kv, f32)
    kvg = []
    for g in range(4):
        cols = np.concatenate([np.arange(g * 64, g * 64 + 64), np.arange(g * 64, g * 64 + 64), 256 + np.arange(g * 64, g * 64 + 64)])
        kvg.append(_pk(wkv[:, cols]))
    wb = np.asarray(w_in_b, f32)[0]
    prm = np.zeros((128, PRM_N), f32)
    prm[:, PRM_GA:PRM_GA + 8] = np.asarray(norm_a_g, f32)[0].reshape(NKC, 128).T
    prm[:, PRM_GKV:PRM_GKV + 8] = np.asarray(kv_norm_g, f32).reshape(NKC, 128).T
    prm[:, PRM_GB:PRM_GB + 8] = np.asarray(norm_b_g, f32)[0].reshape(NKC, 128).T
    prm[:, PRM_GQ0] = np.tile(np.asarray(qnorm_a_g, f32)[0], 2)
    prm[:, PRM_GK0] = np.tile(np.asarray(knorm_a_g, f32)[0], 2)
    prm[0:16, PRM_BF] = np.asarray(b_forget, f32)[0]
    prm[:, PRM_GQ1] = np.tile(np.asarray(qnorm_b_g, f32)[0], 2)
    prm[:, PRM_GK1] = np.tile(np.asarray(knorm_b_g, f32), 2)
    dd = np.arange(128) % 64
    inv_freq = (np.float32(ROPE_THETA) ** (-(2.0 * (dd % 8)) / 16.0)).astype(f32)
    prm[:, PRM_INVF] = np.where(dd < 16, inv_freq, 0.0)
    prm[:, PRM_SIGN] = np.where(dd < 8, -1.0, np.where(dd < 16, 1.0, 0.0))
    prm[:, PRM_M1] = (dd < 8).astype(f32)
    prm[:, PRM_M2] = ((dd >= 8) & (dd < 16)).astype(f32)
    prm[:, PRM_SINK:PRM_SINK + 16] = np.asarray(sinks, f32)[0][None, :]
    return {
        "pos": np.ascontiguousarray(np.asarray(positions, np.int32).reshape(NT, 128).T),
        "posb": np.ascontiguousarray(np.broadcast_to(np.asarray(positions, np.int32)[None, :], (128, S))),
        "prm": prm,
        "wA_hp": np.stack(hp),
        "wA_f": _pk(wa[:, 3072:3088]),
        "wA_o": _pk(np.asarray(w_out_a, f32)[0]),
        "wB_kv": np.stack(kvg),
        "wB_q": _pk(wb[:, 0:1024]),
        "wB_g": _pk(wb[:, 1024:2048]),
        "wB_o": _pk(np.asarray(w_out_b, f32)[0]),
    }


_NC_CACHE = {}


def kernel(x, positions, norm_a_g, w_in_a, b_forget, qnorm_a_g, knorm_a_g, w_out_a, kv_norm_g, w_kv, knorm_b_g,
           norm_b_g, w_in_b, qnorm_b_g, sinks, w_out_b, _dbg=None):
    x = np.asarray(x, np.float32)
    shared = _prep_shared(positions, norm_a_g, w_in_a, b_forget, qnorm_a_g, knorm_a_g, w_out_a, kv_norm_g, w_kv,
                          knorm_b_g, norm_b_g, w_in_b, qnorm_b_g, sinks, w_out_b)
    nb = x.shape[0]
    nc = build(_dbg)
    in_maps = []
    for b in range(nb):
        m = dict(shared)
        m["x"] = np.ascontiguousarray(x[b])
        in_maps.append(m)
    res = run_bass_kernel_spmd(nc, in_maps, core_ids=list(range(nb)))
    return np.stack([np.asarray(r["out"], np.float32) for r in res.results], axis=0)
```

```python
import numpy as np
import concourse.bass as bass
import concourse.mybir as mybir
from concourse.bass_utils import run_bass_kernel_spmd

F32 = mybir.dt.float32
BF16 = mybir.dt.bfloat16
I32 = mybir.dt.int32
ALU = mybir.AluOpType
AF = mybir.ActivationFunctionType
AX = mybir.AxisListType

S = 2048
D = 1024
NT = 16
NKC = 8
HD = 64
EPS = 1e-6
NEG = -30000.0
ROPE_THETA = 500000.0
TWO_PI = 6.283185307179586
PI = 3.141592653589793


class Prog:
    def __init__(self, nc):
        self.nc = nc
        self.eng = {"pe": nc.tensor, "act": nc.scalar, "dve": nc.vector, "pool": nc.gpsimd, "sp": nc.sync}
        self.insts = {e: [] for e in self.eng}
        self.clock = {e: {} for e in self.eng}
        self.snap = {}
        self.lastw = {}
        self.readers = {}
        self.dma_cnt = {}
        self.last_ticket = {}
        self.ever_written = set()
        self.read_before_write = set()

    def _add(self, eng, fn, R, W, dma=None, extra_waits=()):
        clk = self.clock[eng]
        deps = []
        for r in R:
            if r not in self.ever_written:
                self.read_before_write.add(r)
            excl = isinstance(r, tuple) and r[0] == "ps"
            w = self.lastw.get(r)
            if w is not None:
                deps.append((w, True))
            if excl:
                for rd in self.readers.get(r, ()):
                    deps.append((rd, False))
        for wk in W:
            w = self.lastw.get(wk)
            if w is not None:
                deps.append((w, False))
            for rd in self.readers.get(wk, ()):
                deps.append((rd, False))
        for t in extra_waits:
            deps.append((t, True))
        waits = {}
        for (t, raw) in deps:
            te, ti = t
            if te == eng and eng != "pool" and (not raw or eng in ("pe", "sp")):
                continue
            if clk.get(te, 0) >= ti:
                continue
            if waits.get(te, 0) < ti:
                waits[te] = ti
        for te, ti in waits.items():
            for k, v in self.snap[(te, ti)].items():
                if clk.get(k, 0) < v:
                    clk[k] = v
        if dma is not None:
            n = self.dma_cnt.get(dma, 0) + 1
            self.dma_cnt[dma] = n
            ticket = (("dma", dma), n)
        else:
            ticket = (eng, len(self.insts[eng]) + 1)
        sn = dict(clk)
        sn[ticket[0]] = ticket[1]
        self.snap[ticket] = sn
        self.insts[eng].append({"fn": fn, "waits": sorted(waits.items(), key=str), "ticket": ticket, "dma": dma})
        if fn is not None:
            self.last_ticket[ticket[0]] = ticket
        for r in R:
            excl = isinstance(r, tuple) and r[0] == "ps"
            if excl:
                self.lastw[r] = ticket
                self.readers[r] = []
            else:
                self.readers.setdefault(r, []).append(ticket)
        for wk in W:
            self.lastw[wk] = ticket
            self.readers[wk] = []
            self.ever_written.add(wk)
        for r in R:
            if isinstance(r, tuple) and r[0] == "ps":
                self.ever_written.add(r)
        return ticket

    def op(self, eng, fn, R=(), W=()):
        return self._add(eng, fn, R, W)

    def dma(self, fn, R, W, sem, queue="sp"):
        return self._add(queue, fn, R, W, dma=sem)

    def barrier(self):
        tickets = [t for t in self.last_ticket.values()]
        for e in ("pe", "act", "dve", "pool", "sp"):
            ws = [t for t in tickets if t[0] != e]
            self._add(e, None, (), (), extra_waits=ws)
        self.lastw = {}
        self.readers = {}

    def emit(self):
        nc = self.nc
        sems = {}
        for e in ("pe", "act", "dve", "pool"):
            sems[e] = nc.alloc_semaphore("s_" + e)
        for name in self.dma_cnt:
            sems[("dma", name)] = nc.alloc_semaphore("d_" + name)
        signaled = set()
        for e, lst in self.insts.items():
            for ins in lst:
                for te, ti in ins["waits"]:
                    signaled.add((te, ti))
        val = {}
        for e, lst in self.insts.items():
            c = 0
            for ins in lst:
                t = ins["ticket"]
                if t[0] == e and t in signaled:
                    if ins["fn"] is None:
                        raise RuntimeError("barrier nop cannot signal")
                    c += 1
                    val[t] = c
        for e, lst in self.insts.items():
            h = self.eng[e]
            for ins in lst:
                for te, ti in ins["waits"]:
                    if isinstance(te, tuple):
                        h.wait_ge(sems[te], 16 * ti)
                    else:
                        h.wait_ge(sems[te], val[(te, ti)])
                if ins["fn"] is None:
                    continue
                bi = ins["fn"]()
                t = ins["ticket"]
                if ins["dma"] is not None:
                    bi.then_inc(sems[t[0]], 16)
                elif t in signaled:
                    bi.then_inc(sems[e], 1)


def call(f, *a, **k):
    return lambda: f(*a, **k)


class Mem:
    def __init__(self, nc, base, limit):
        self.nc, self.off, self.limit, self.n = nc, base, limit, 0

    def alloc(self, name, shape, dtype):
        esz = 4 if dtype in (F32, I32) else 2
        nbytes = esz
        for s in shape[1:]:
            nbytes *= s
        off = (self.off + 63) // 64 * 64
        if off + nbytes > self.limit:
            raise RuntimeError(f"SBUF overflow allocating {name}: {off}+{nbytes} > {self.limit}")
        self.n += 1
        t = self.nc.alloc_sbuf_tensor_at(f"{name}_{self.n}", list(shape), dtype, offset=off)
        self.off = off + nbytes
        return t


class Rot:
    def __init__(self, items):
        self.items, self.i = list(items), 0

    def next(self):
        x = self.items[self.i % len(self.items)]
        self.i += 1
        return x


HH_ORDER = (0, 1)
PRM_GA, PRM_GKV, PRM_GB, PRM_GQ0, PRM_GK0, PRM_BF, PRM_GQ1, PRM_GK1, PRM_INVF, PRM_SIGN, PRM_M1, PRM_M2, PRM_SINK, PRM_N = 0, 8, 16, 24, 25, 26, 27, 28, 29, 30, 31, 32, 33, 49


def build(dbg=None):
    nc = bass.Bass("TRN2", target_bir_lowering=False)
    dram = {}

    def din(name, shape, dt=F32):
        dram[name] = nc.dram_tensor(name, list(shape), dt, kind="ExternalInput").ap()
        return dram[name]

    x_d = din("x", [S, D])
    pos_d = din("pos", [128, NT], I32)
    prm_d = din("prm", [128, PRM_N])
    wA_hp = din("wA_hp", [8, 128, NKC, 512])
    wA_f = din("wA_f", [128, NKC, 16])
    wA_o = din("wA_o", [128, NKC, D])
    wB_kv = din("wB_kv", [4, 128, NKC, 192])
    posb_d = din("posb", [128, S], I32)
    wB_q = din("wB_q", [128, NKC, D])
    wB_g = din("wB_g", [128, NKC, D])
    wB_o = din("wB_o", [128, NKC, D])
    out_d = nc.dram_tensor("out", [S, D], F32, kind="ExternalOutput").ap()

    P = Prog(nc)
    V, A, T, G, SP = nc.vector, nc.scalar, nc.tensor, nc.gpsimd, nc.sync
    mem = Mem(nc, 16640, 229376)
    psum = nc.alloc_psum_tensor("psum", [128, 8 * 512], F32)

    def bank(b, n=512, p0=0, p1=128, c0=0):
        return psum[p0:p1, b * 512 + c0:b * 512 + c0 + n]

    def bank_bf(b):
        return psum[:, b * 512:(b + 1) * 512].bitcast(BF16)

    ident = mem.alloc("ident", [128, 128], BF16)
    identf = mem.alloc("identf", [128, 128], F32)
    bd = mem.alloc("bd", [128, 128], BF16)
    maskC = mem.alloc("maskC", [128, 128], BF16)
    maskW = mem.alloc("maskW", [128, 128], BF16)
    mtmp = mem.alloc("mtmp", [128, 128], F32)
    prm = mem.alloc("prm", [128, PRM_N], F32)
    epsT = mem.alloc("epsT", [128, 1], F32)
    gq0s = mem.alloc("gq0s", [128, 1], F32)
    negb = mem.alloc("negb", [128, 1], F32)
    small = mem.alloc("small", [128, 64], F32)
    posi = mem.alloc("posi", [128, NT], I32)
    es_bc = mem.alloc("es_bc", [128, 16], F32)
    gq1s = mem.alloc("gq1s", [128, 1], F32)
    wreg0 = mem.off
    WS = [mem.alloc(f"WS{i}", [128, NKC, 512], F32) for i in range(2)]
    WB = [mem.alloc(f"WB{i}", [128, NKC, 512], BF16) for i in range(2)]
    XS = [mem.alloc(f"XS{i}", [128, D], F32) for i in range(2)]
    wreg1 = mem.off
    l0_base = mem.off


    def dump(ap, row0, n, parts=128):
        P.barrier()
        P.op("dve", call(V.tensor_copy, XS[0][0:parts, 0:n], ap), W=["dbgxs"])
        P.dma(call(SP.dma_start, out=out_d[row0:row0 + parts, 0:n], in_=XS[0][0:parts, 0:n]), R=["dbgxs"], W=["dbgout"], sem="outst")
        P.barrier()

    def finish():
        P.barrier()
        P.emit()
        return nc

    def mk_consts():
        P.op("pool", call(G.memset, mtmp[:], 0.0), W=["mtmp"])
        P.op("pool", call(G.affine_select, out=mtmp[:], in_=mtmp[:], compare_op=ALU.not_equal, fill=1.0,
                                             base=0, pattern=[[-1, 128]], channel_multiplier=1), R=["mtmp"], W=["mtmp"])
        P.op("pool", call(G.tensor_copy, ident[:], mtmp[:]), R=["mtmp"], W=["ident"])
        P.op("pool", call(G.tensor_copy, identf[:], mtmp[:]), R=["mtmp"], W=["identf"])
        P.op("pool", call(G.memset, mtmp[:], 0.0), R=["mtmp"], W=["mtmp"])
        P.op("pool", call(G.affine_select, out=mtmp[:], in_=mtmp[:], compare_op=ALU.is_ge, fill=NEG,
                                             base=0, pattern=[[1, 128]], channel_multiplier=-1), R=["mtmp"], W=["mtmp"])
        P.op("pool", call(G.tensor_copy, maskC[:], mtmp[:]), R=["mtmp"], W=["maskC"])
        P.op("pool", call(G.memset, mtmp[:], 0.0), R=["mtmp"], W=["mtmp"])
        P.op("pool", call(G.affine_select, out=mtmp[:], in_=mtmp[:], compare_op=ALU.is_gt, fill=NEG,
                                             base=0, pattern=[[-1, 128]], channel_multiplier=1), R=["mtmp"], W=["mtmp"])
        P.op("pool", call(G.tensor_copy, maskW[:], mtmp[:]), R=["mtmp"], W=["maskW"])
        P.op("pool", call(G.memset, bd[:], 0.0), W=["bd"])
        P.op("pool", call(G.memset, bd[0:64, 0:64], 1.0 / 64), W=["bd"])
        P.op("pool", call(G.memset, bd[64:128, 64:128], 1.0 / 64), W=["bd"])
        P.op("pool", call(G.memset, epsT[:], EPS), W=["epsT"])
        P.dma(call(SP.dma_start, out=prm[:], in_=prm_d[:, :]), R=[], W=["prm"], sem="prm")
        P.dma(call(SP.dma_start, out=posi[:], in_=pos_d[:, :]), R=[], W=["posi"], sem="posi")
        P.op("dve", call(V.tensor_single_scalar, gq0s[:], prm[:, PRM_GQ0:PRM_GQ0 + 1], 0.125, ALU.mult),
             R=["prm"], W=["gq0s"])
        P.op("dve", call(V.tensor_single_scalar, negb[:], prm[:, PRM_BF:PRM_BF + 1], -1.0, ALU.mult),
             R=["prm"], W=["negb"])

    mk_consts()

    wrot = {"ws": 0, "wb": 0}

    def w_dma(src_ap, ncols, nkc=NKC, ws_list=None):
        ws_list = ws_list or WS
        si = wrot["ws"] % len(ws_list)
        wrot["ws"] += 1
        ws = ws_list[si]
        kws = ("ws", si)
        P.dma(call(SP.dma_start, out=ws[:, 0:nkc, 0:ncols], in_=src_ap), R=[], W=[kws], sem=f"ws{si}")
        return (ws, kws, ncols, nkc)

    def w_cast(h, gain_col, wb_list=None):
        ws, kws, ncols, nkc = h
        wb_list = wb_list or WB
        bi = wrot["wb"] % len(wb_list)
        wrot["wb"] += 1
        wb = wb_list[bi]
        kwb = ("wb", bi)
        if gain_col is None:
            half = nkc // 2
            P.op("dve", call(V.tensor_copy, wb[:, 0:half, 0:ncols], ws[:, 0:half, 0:ncols]), R=[kws], W=[kwb])
            P.op("dve", call(V.tensor_copy, wb[:, half:nkc, 0:ncols], ws[:, half:nkc, 0:ncols]), R=[kws], W=[kwb])
        else:
            for kc in range(nkc):
                P.op("dve", call(V.tensor_single_scalar, wb[:, kc, 0:ncols], ws[:, kc, 0:ncols],
                                 prm[:, gain_col + kc:gain_col + kc + 1], ALU.mult),
                     R=[kws, "prm"], W=[kwb])
        return wb, kwb

    def load_w(src_ap, ncols, gain_col, nkc=NKC, ws_list=None, wb_list=None):
        return w_cast(w_dma(src_ap, ncols, nkc, ws_list), gain_col, wb_list)

    def norm_transpose(src_fn, uT, uT_key, ubuf, sqbuf, banks):
        rb = Rot(banks)
        for t in range(NT):
            xap, xkey = src_fn(t)
            u = ubuf[t % 2]
            ku = ("u", t % 2)
            ss = small[:, (t % 2) * 4:(t % 2) * 4 + 1]
            rs = small[:, (t % 2) * 4 + 1:(t % 2) * 4 + 2]
            kss = ("ss", t % 2)
            P.op("dve", call(V.memset, ss, 0.0), W=[kss])
            P.op("act", call(A.activation, out=sqbuf[:], in_=xap, func=AF.Square, accum_out=ss),
                 R=[xkey, kss], W=["sqbuf", kss])
            P.op("act", call(A.activation, out=rs, in_=ss, func=AF.Ln, bias=epsT[:, 0:1], scale=1.0 / D),
                 R=[kss, "epsT"], W=[("rs", t % 2)])
            P.op("act", call(A.activation, out=rs, in_=rs, func=AF.Exp, scale=-0.5), R=[("rs", t % 2)], W=[("rs", t % 2)])
            P.op("dve", call(V.tensor_single_scalar, u[:], xap, rs, ALU.mult),
                 R=[xkey, ("rs", t % 2)], W=[ku])
            b = rb.next()
            pb = bank_bf(b)
            for kc in range(NKC):
                P.op("pe", call(T.transpose, pb[:, kc * 128:(kc + 1) * 128], u[:, kc * 128:(kc + 1) * 128], ident[:]),
                     R=[ku, "ident"], W=[("ps", b)])
            P.op("dve", call(V.tensor_copy, uT[:, :, t * 128:(t + 1) * 128],
                                                         pb.rearrange("p (k n) -> p k n", n=128)),
                 R=[("ps", b)], W=[(uT_key, t)])

    ogT = mem.alloc("ogT", [128, NKC, S], BF16)
    l0_after_og = mem.off
    uT0 = mem.alloc("uT0", [128, NKC, S], BF16)
    qg0 = mem.off
    QTa = [mem.alloc(f"QTa{i}", [128, S], BF16) for i in range(4)]
    gsb = [mem.alloc(f"gs{i}", [128, S], BF16) for i in range(2)]
    qg1 = mem.off
    KTa = [mem.alloc(f"KTa{i}", [128, S], BF16) for i in range(4)]
    Vp = [mem.alloc(f"Vp{i}", [128, NT, 192], BF16) for i in range(2)]
    c3t = mem.alloc("c3t", [128, S], BF16)
    c3 = [c3t[32 * i:32 * i + 16, :] for i in range(3)]
    negc_tok = mem.alloc("negc_tok", [128, NT, 16], F32)
    regx = mem.off
    mem.off = l0_base
    xs4 = [mem.alloc(f"xs4_{i}", [128, D], F32) for i in range(4)]
    ubuf = [mem.alloc(f"ubuf{i}", [128, D], BF16) for i in range(2)]
    sqbuf = mem.alloc("sqbuf", [128, D], F32)
    assert mem.off <= l0_after_og
    mem.off = l0_base
    lfA = mem.alloc("lfA", [128, S], F32)
    lfB = mem.alloc("lfB", [128, S], F32)
    onesS = mem.alloc("onesS", [128, S], F32)
    cmid0 = mem.alloc("cmid0", [128, S], BF16)
    assert mem.off <= l0_after_og
    mem.off = regx
    PT = [mem.alloc(f"PT{i}", [128, 512], BF16) for i in range(3)]
    NQF = 3
    qf = [mem.alloc(f"qf{i}", [128, 512], F32) for i in range(NQF)]
    sqb = [mem.alloc(f"sqb{i}", [128, 512], BF16) for i in range(NQF)]
    rsb = [mem.alloc(f"rsb{i}", [128, 512], F32) for i in range(NQF)]
    rinv = [mem.alloc(f"rinv{i}", [128, 512], F32) for i in range(2)]
    otmp = [mem.alloc(f"otmp{i}", [128, 512], F32) for i in range(2)]
    l0_end_d = mem.off

    events = []
    seq = [0]

    def at(time, fn):
        seq[0] += 1
        events.append((time, seq[0], fn))

    uT0_keys = [("uT0", t) for t in range(NT)]
    trb = Rot([0, 1, 2, 3])

    def a_tile(t, TA):
        st = {}
        xs = xs4[t % 4]
        kxs = ("xs4", t % 4)
        u = ubuf[t % 2]
        ku = ("u", t % 2)
        ss = small[:, (t % 4) * 4:(t % 4) * 4 + 1]
        rs = small[:, (t % 4) * 4 + 1:(t % 4) * 4 + 2]
        kss, krs = ("ss", t % 4), ("rs", t % 4)

        def ldx():
            P.dma(call(SP.dma_start, out=xs[:], in_=x_d[t * 128:(t + 1) * 128, :]), R=[], W=[kxs], sem=f"xs4_{t % 4}")

        def n1():
            P.op("dve", call(V.memset, ss, 0.0), W=[kss])
            P.op("act", call(A.activation, out=sqbuf[:], in_=xs[:], func=AF.Square, accum_out=ss), R=[kxs, kss], W=["sqbuf", kss])

        def n2():
            P.op("act", call(A.activation, out=rs, in_=ss, func=AF.Ln, bias=epsT[:, 0:1], scale=1.0 / D), R=[kss, "epsT"], W=[krs])

        def n3():
            P.op("act", call(A.activation, out=rs, in_=rs, func=AF.Exp, scale=-0.5), R=[krs], W=[krs])

        def n4():
            P.op("dve", call(V.tensor_single_scalar, u[:], xs[:], rs, ALU.mult), R=[kxs, krs], W=[ku])

        def n5():
            b = st["b"] = trb.next()
            pb = bank_bf(b)
            for kc in range(NKC):
                P.op("pe", call(T.transpose, pb[:, kc * 128:(kc + 1) * 128], u[:, kc * 128:(kc + 1) * 128], ident[:]),
                     R=[ku, "ident"], W=[("ps", b)])

        def n6():
            b = st["b"]
            P.op("dve", call(V.tensor_copy, uT0[:, :, t * 128:(t + 1) * 128], bank_bf(b).rearrange("p (k n) -> p k n", n=128)),
                 R=[("ps", b)], W=[("uT0", t)])
        at(TA - 7.9, ldx)
        for off, fn in ((0, n1), (1, n2), (2, n3), (4, n4), (6, n5), (8, n6)):
            at(TA + off, fn)

    for t in range(NT):
        a_tile(t, -220.0 + 3.5 * t)

    dmaH, dmaF, dmaO = {}, {}, {}

    def w_prologue():
        dmaH[0] = w_dma(wA_hp[0], 512)
        dmaF["h"] = w_dma(wA_f[:, :, :], 16)
    at(-232.0, w_prologue)

    holdf = {}

    def c_load():
        holdf["w"] = w_cast(dmaF["h"], PRM_GA)
        P.op("pool", call(G.memset, onesS[0:16, :], 1.0), R=uT0_keys, W=["onesS"])
        for i in range(4):
            P.op("pool", call(G.memset, KTa[i][64:67, :], 1.0), W=[("KTa_aug", i)])
        for i in range(2):
            P.op("pool", call(G.memset, Vp[i][:, :, 64:128], 1.0), W=[("Vp_ones", i)])
    at(-158.0, c_load)

    def c_f(tc):
        wbf, kwbf = holdf["w"]
        b = 3 + tc % 2
        for kc in range(NKC):
            P.op("pe", call(T.matmul, bank(b, 512, 0, 16), wbf[:, kc, 0:16], uT0[:, kc, tc * 512:(tc + 1) * 512],
                            start=(kc == 0), stop=(kc == NKC - 1)), R=[kwbf] + uT0_keys, W=[("ps", b)])
        P.op("act", call(A.activation, out=lfA[0:16, tc * 512:(tc + 1) * 512], in_=bank(b, 512, 0, 16), func=AF.Exp,
                         bias=negb[0:16, 0:1], scale=-1.0), R=[("ps", b), "negb"], W=[("lfA", tc)])
    for tc in range(4):
        at(-150.0 + 1.5 * tc, lambda tc=tc: c_f(tc))

    def c_ln():
        P.op("act", call(A.activation, out=lfA[0:16, :], in_=lfA[0:16, :], func=AF.Ln, bias=1.0, scale=1.0),
             R=[("lfA", tc) for tc in range(4)], W=["lfA_all"])
    at(-142.0, c_ln)

    def c_scan():
        P.op("dve", call(V.tensor_tensor_scan, lfB[0:16, :], onesS[0:16, :], lfA[0:16, :], 0.0, ALU.mult, ALU.add),
             R=["lfA_all", "onesS"], W=["lfB"])
    at(-138.0, c_scan)

    def c_tr(t):
        b = t % 2
        P.op("pe", call(T.transpose, bank(b, 16), lfB[0:16, t * 128:(t + 1) * 128], identf[0:16, 0:16]), R=["lfB", "identf"], W=[("ps", b)])
        P.op("dve", call(V.tensor_copy, negc_tok[:, t, :], bank(b, 16)), R=[("ps", b)], W=["negc_tok"])
    for t in range(NT):
        at(-132.0 + 0.5 * t, lambda t=t: c_tr(t))

    def c_split():
        P.op("dve", call(V.tensor_single_scalar, c3[0], lfB[0:16, :], -1.0, ALU.mult), R=["lfB"], W=["c3_0"])
        P.op("dve", call(V.scalar_tensor_tensor, lfA[0:16, :], lfB[0:16, :], -1.0, c3[0], ALU.mult, ALU.subtract),
             R=["lfB", "c3_0", "lfA_all"], W=["lfA_r1"])
        P.op("dve", call(V.tensor_copy, cmid0[0:16, :], lfA[0:16, :]), R=["lfA_r1"], W=["cmid0"])
        P.op("dve", call(V.tensor_copy, c3[1], cmid0[0:16, :]), R=["cmid0"], W=["c3_1"])
        P.op("dve", call(V.tensor_tensor, lfB[0:16, :], lfA[0:16, :], cmid0[0:16, :], ALU.subtract),
             R=["lfA_r1", "cmid0", "negc_tok"], W=["lfB_r2"])
        P.op("dve", call(V.tensor_copy, c3[2], lfB[0:16, :]), R=["lfB_r2"], W=["c3_2"])
    at(-122.0, c_split)

    s_banks = Rot([0, 1, 2])
    o_banks = Rot([3, 4])
    p_banks = Rot([5, 6, 7])
    pt_rot = Rot([0, 1, 2])
    sc_rot = Rot(list(range(NQF)))
    ev_rot = Rot([0, 1])

    def sched_proj(c, base):
        par = c % 2
        gs, vp = gsb[par], Vp[par]
        kgs, kvp = ("gs", par), ("Vp", par)
        hold = {}

        def wload():
            hold["wb"], hold["kwb"] = w_cast(dmaH[c], PRM_GA)
            if c + 1 < 8:
                dmaH[c + 1] = w_dma(wA_hp[c + 1], 512)
            if c == 7:
                dmaO[0] = w_dma(wA_o[:, :, 0:512], 512)
                dmaO[1] = w_dma(wA_o[:, :, 512:1024], 512)
        at(base - 0.5, wload)

        def v_block(t, T0):
            st = {}

            def a():
                b = st["b"] = p_banks.next()
                wb, kwb = hold["wb"], hold["kwb"]
                for kc in range(NKC):
                    P.op("pe", call(T.matmul, bank(b, 128), uT0[:, kc, t * 128:(t + 1) * 128], wb[:, kc, 384:512],
                                    start=(kc == 0), stop=(kc == NKC - 1)), R=[kwb, ("uT0", t)], W=[("ps", b)])

            def bfn():
                b = st["b"]
                P.op("dve", call(V.tensor_copy, vp[:, t, :].rearrange("p (a n) -> p a n", n=64)[:, 0:3:2, :],
                                 bank(b, 128).rearrange("p (a n) -> p a n", n=64)),
                     R=[("ps", b), ("Vp_ones", par)], W=[kvp])
            at(T0, a)
            at(T0 + 1, bfn)

        def g_block(tc, T0):
            st = {}

            def a():
                b = st["b"] = p_banks.next()
                wb, kwb = hold["wb"], hold["kwb"]
                for kc in range(NKC):
                    P.op("pe", call(T.matmul, bank(b), wb[:, kc, 256:384], uT0[:, kc, tc * 512:(tc + 1) * 512],
                                    start=(kc == 0), stop=(kc == NKC - 1)), R=[kwb] + uT0_keys[tc * 4:tc * 4 + 4], W=[("ps", b)])

            def bfn():
                b = st["b"]
                P.op("dve", call(V.tensor_copy, gs[:, tc * 512:(tc + 1) * 512], bank(b)), R=[("ps", b)], W=[("gspre", par, tc), kgs])
            at(T0, a)
            at(T0 + 3, bfn)

        def qk_block(which, tc, T0):
            st = {}
            dst = QTa if which == 0 else KTa
            gcol = gq0s if which == 0 else prm[:, PRM_GK0:PRM_GK0 + 1]
            gkey = "gq0s" if which == 0 else "prm"

            def a():
                b = st["b"] = p_banks.next()
                wb, kwb = hold["wb"], hold["kwb"]
                for kc in range(NKC):
                    P.op("pe", call(T.matmul, bank(b), wb[:, kc, which * 128:(which + 1) * 128], uT0[:, kc, tc * 512:(tc + 1) * 512],
                                    start=(kc == 0), stop=(kc == NKC - 1)), R=[kwb] + uT0_keys[tc * 4:tc * 4 + 4], W=[("ps", b)])

            def a2():
                si = st["si"] = sc_rot.next()
                P.op("dve", call(V.tensor_copy, qf[si][:], bank(st["b"])), R=[("ps", st["b"])], W=[("qf", si)])
                P.op("pool", call(G.tensor_tensor, sqb[si][:], qf[si][:], qf[si][:], ALU.mult), R=[("qf", si)], W=[("sqb", si)])

            def cfn():
                b2 = st["b2"] = p_banks.next()
                P.op("pe", call(T.matmul, bank(b2), bd[:], sqb[st["si"]][:], start=True, stop=True), R=[("sqb", st["si"]), "bd"], W=[("ps", b2)])

            def d1():
                si = st["si"]
                P.op("act", call(A.activation, out=rsb[si][:], in_=bank(st["b2"]), func=AF.Ln, bias=epsT[:, 0:1]),
                     R=[("ps", st["b2"]), "epsT"], W=[("rsb", si)])

            def d2():
                si = st["si"]
                P.op("act", call(A.activation, out=rsb[si][:], in_=rsb[si][:], func=AF.Exp, scale=-0.5), R=[("rsb", si)], W=[("rsb", si)])

            def e():
                si = st["si"]
                for hh in range(2):
                    pp = slice(hh * 64, hh * 64 + 64)
                    P.op("dve", call(V.scalar_tensor_tensor, dst[2 * par + hh][0:64, tc * 512:(tc + 1) * 512], qf[si][pp, :],
                                     gcol[pp, 0:1], rsb[si][pp, :], ALU.mult, ALU.mult),
                         R=[("qf", si), ("rsb", si), gkey], W=[("QK", which, 2 * par + hh, tc)])
            for off, fn in ((0, a), (3, a2), (9, cfn), (11, d1), (13, d2), (15, e)):
                at(T0 + off, fn)

        blocks = [("g", tc) for tc in range(4)]
        vt = 0
        for tc in range(4):
            for which in range(2):
                blocks.append(("qk", which, tc))
                for _ in range(2):
                    blocks.append(("v", vt))
                    vt += 1
        for k, blk in enumerate(blocks):
            T0 = base + 1 + k * 2.5
            if blk[0] == "g":
                g_block(blk[1], T0)
            elif blk[0] == "v":
                v_block(blk[1], T0)
            else:
                qk_block(blk[1], blk[2], T0)

        def silu_all():
            P.op("act", call(A.activation, out=gs[:, :], in_=gs[:, :], func=AF.Silu), R=[("gspre", par, tc) for tc in range(4)], W=[kgs])
        at(base + 84.0, silu_all)

        def aug():
            for hh in range(2):
                h = 2 * c + hh
                for r in range(3):
                    P.dma(call(SP.dma_start, out=QTa[2 * par + hh][64 + r:65 + r, :], in_=c3t[32 * r + h:32 * r + h + 1, :]),
                          R=[f"c3_{r}"], W=[("Qaug", 2 * par + hh, r)], sem=f"qaug{2 * par + hh}_{r}")
        at(max(base + 0.2, -121.0), aug)

    def sched_attn(c, base):
        par = c % 2
        gs, vp = gsb[par], Vp[par]
        kgs, kvp = ("gs", par), ("Vp", par)
        tiles = [(hh, qc, j) for hh in HH_ORDER for qc in range(4) for j in range(4 * qc + 4)]
        info = {}
        obank = {}
        fin = {}
        LOOK = 2

        def emit_S(i):
            hh, qc, j = tiles[i]
            h = 2 * c + hh
            sl = 2 * par + hh
            qt, kt = QTa[sl], KTa[sl]
            qs = max(qc * 512, j * 128)
            n = (qc + 1) * 512 - qs
            sb = s_banks.next()
            diag = j >= 4 * qc
            qk_keys = [("QK", 0, sl, qc), ("Qaug", sl, 0), ("Qaug", sl, 1), ("Qaug", sl, 2), ("QK", 1, sl, j // 4), ("KTa_aug", sl)]
            P.op("pe", call(T.matmul, bank(sb, n), kt[0:67, j * 128:(j + 1) * 128], qt[0:67, qs:qs + n], start=True, stop=(not diag)),
                 R=qk_keys, W=[("ps", sb)])
            if diag:
                P.op("pe", call(T.matmul, bank(sb, 128), ident[:], maskC[:], start=False, stop=True), R=["ident", "maskC"], W=[("ps", sb)])
            pi = pt_rot.next()
            P.op("act", call(A.activation, out=PT[pi][:, 0:n], in_=bank(sb, n), func=AF.Exp, bias=negc_tok[:, j, h:h + 1], scale=1.0),
                 R=[("ps", sb), "negc_tok"], W=[("PT", pi)])
            info[i] = (pi, n, qs - qc * 512)

        def emit_PV(i):
            hh, qc, j = tiles[i]
            pi, n, coff = info[i]
            vcol = 0 if hh == 0 else 64
            o0, s0 = (0, 64) if hh == 0 else (64, 0)
            nj = 4 * qc + 4
            if j == 0:
                obank[(hh, qc)] = o_banks.next()
            ob = obank[(hh, qc)]
            P.op("pe", call(T.matmul, bank(ob, n, c0=coff), vp[:, j, vcol:vcol + 128], PT[pi][:, 0:n], start=(j == 0), stop=(j == nj - 1)),
                 R=[("PT", pi), kvp, ("Vp_ones", par)], W=[("ps", ob)])
            if j == nj - 1:
                ei = ev_rot.next()
                P.op("act", call(A.activation, out=rinv[ei][s0:s0 + 64, :], in_=bank(ob, 512, s0, s0 + 64), func=AF.Ln), R=[("ps", ob)], W=[("rinv", ei)])
                P.op("act", call(A.activation, out=rinv[ei][o0:o0 + 64, :], in_=rinv[ei][s0:s0 + 64, :], func=AF.Exp, scale=-1.0),
                     R=[("rinv", ei)], W=[("rinv", ei)])
                fin[(hh, qc)] = (ei, ob, o0)

        def emit_fin(hh, qc):
            ei, ob, o0 = fin[(hh, qc)]
            P.op("dve", call(V.tensor_tensor, otmp[ei][o0:o0 + 64, :], bank(ob, 512, o0, o0 + 64), rinv[ei][o0:o0 + 64, :], ALU.mult),
                 R=[("ps", ob), ("rinv", ei)], W=[("otmp", ei)])
            P.op("pool", call(G.tensor_tensor, ogT[o0:o0 + 64, c, qc * 512:(qc + 1) * 512], otmp[ei][o0:o0 + 64, :],
                              gs[o0:o0 + 64, qc * 512:(qc + 1) * 512], ALU.mult),
                 R=[("otmp", ei), kgs], W=[("ogT", c, qc)])

        for i in range(len(tiles)):
            at(base + i, lambda i=i: emit_S(i))
            at(base + i + LOOK + 0.1, lambda i=i: emit_PV(i))
            hh_, qc_, j_ = tiles[i]
            if j_ == 4 * qc_ + 3:
                at(base + i + LOOK + 3.6, lambda hh_=hh_, qc_=qc_: emit_fin(hh_, qc_))

    sched_proj(0, -160.0)
    for c in range(8):
        sched_attn(c, c * 100.0)
        if c + 1 < 8:
            sched_proj(c + 1, c * 100.0)
    events.sort(key=lambda e: (e[0], e[1]))
    for _, _, fn in events:
        fn()

    P.barrier()

    mem.off = l0_after_og
    h1 = mem.alloc("h1", [128, NT, D], F32)
    l1_base = mem.off
    mem.off = l0_base
    uT1 = mem.alloc("uT1", [128, NKC, S], BF16)
    assert mem.off <= l0_after_og
    mem.off = l1_base
    KTs1 = mem.alloc("KTs1", [128, S], BF16)
    V1g = mem.alloc("V1g", [128, NT, 192], BF16)
    Cfm = mem.alloc("Cfm", [128, S], F32)
    Sfm = mem.alloc("Sfm", [128, S], F32)
    grp0 = mem.off
    QT1 = mem.alloc("QT1", [128, 2, S], BF16)
    gs1 = mem.alloc("gs1", [128, 2, S], BF16)
    og1 = mem.alloc("og1", [128, 2, S], BF16)
    mem.off = grp0
    posb = mem.alloc("posb", [128, S], I32)
    angA = mem.alloc("angA", [128, S], F32)
    angB = mem.alloc("angB", [128, S], F32)
    mem.off = l1_base
    ubuf1 = [mem.alloc(f"ubuf1_{i}", [128, D], BF16) for i in range(2)]
    sq1 = mem.alloc("sq1", [128, D], F32)
    assert mem.off <= l1_base + 4096 + 6144

    ev2 = []
    sq2 = [0]

    def at2(time, fn):
        sq2[0] += 1
        ev2.append((time, sq2[0], fn))

    hold_o = {}

    def ld_wo():
        hold_o[0] = w_cast(dmaO[0], None)
        hold_o[1] = w_cast(dmaO[1], None)
    at2(-20, ld_wo)
    eb = Rot([0, 1, 2, 3, 4, 5])
    tb2 = Rot([6, 7])
    uT1_keys = [("uT1", t) for t in range(NT)]

    def e_tile(t):
        st = {}
        xs = XS[t % 2]
        kxs = ("xs", t % 2)

        def ldx():
            P.dma(call(SP.dma_start, out=xs[:], in_=x_d[t * 128:(t + 1) * 128, :]), R=[], W=[kxs], sem=f"xs{t % 2}")

        def mm():
            st["b"] = [eb.next(), eb.next()]
            for hf in range(2):
                w, kw = hold_o[hf]
                for kc in range(NKC):
                    P.op("pe", call(T.matmul, bank(st["b"][hf]), ogT[:, kc, t * 128:(t + 1) * 128], w[:, kc, :],
                                    start=(kc == 0), stop=(kc == NKC - 1)), R=[kw, ("ogt", t)], W=[("ps", st["b"][hf])])

        def add():
            for hf in range(2):
                P.op("dve", call(V.tensor_tensor, h1[:, t, hf * 512:(hf + 1) * 512], bank(st["b"][hf]), xs[:, hf * 512:(hf + 1) * 512], ALU.add),
                     R=[("ps", st["b"][hf]), kxs], W=[("h1", t, hf)])

        u = ubuf1[t % 2]
        ku = ("u1", t % 2)
        ss = small[:, (t % 2) * 4:(t % 2) * 4 + 1]
        rs = small[:, (t % 2) * 4 + 1:(t % 2) * 4 + 2]
        kss, krs = ("ss", t % 2), ("rs", t % 2)

        def n1():
            P.op("dve", call(V.memset, ss, 0.0), W=[kss])
            P.op("act", call(A.activation, out=sq1[:], in_=h1[:, t, :], func=AF.Square, accum_out=ss),
                 R=[("h1", t, 0), ("h1", t, 1), kss], W=["sq1", kss])

        def n2():
            P.op("act", call(A.activation, out=rs, in_=ss, func=AF.Ln, bias=epsT[:, 0:1], scale=1.0 / D), R=[kss, "epsT"], W=[krs])

        def n3():
            P.op("act", call(A.activation, out=rs, in_=rs, func=AF.Exp, scale=-0.5), R=[krs], W=[krs])

        def n4():
            P.op("dve", call(V.tensor_single_scalar, u[:], h1[:, t, :], rs, ALU.mult), R=[("h1", t, 0), ("h1", t, 1), krs], W=[ku])

        def n5():
            b = st["tb"] = tb2.next()
            pb = bank_bf(b)
            for kc in range(NKC):
                P.op("pe", call(T.transpose, pb[:, kc * 128:(kc + 1) * 128], u[:, kc * 128:(kc + 1) * 128], ident[:]),
                     R=[ku, "ident"], W=[("ps", b)])

        def n6():
            b = st["tb"]
            P.op("dve", call(V.tensor_copy, uT1[:, :, t * 128:(t + 1) * 128], bank_bf(b).rearrange("p (k n) -> p k n", n=128)),
                 R=[("ps", b)], W=[("uT1", t), ("ogt", t)])
        T0 = 5.0 * t
        at2(T0 - 10 + 3.5, ldx)
        at2(T0, mm)
        at2(T0 + 3, add)
        at2(T0 + 5, n1)
        at2(T0 + 7, n2)
        at2(T0 + 8, n3)
        at2(T0 + 9, n4)
        at2(T0 + 11, n5)
        at2(T0 + 14, n6)

    for t in range(NT):
        e_tile(t)

    tbl = []
    tbl.append(lambda: P.dma(call(G.dma_start, out=angA[:], in_=posb_d[:, :]), R=[], W=["angA"], sem="posb", queue="pool"))
    tbl.append(lambda: P.op("dve", call(V.tensor_single_scalar, angA[:], angA[:], prm[:, PRM_INVF:PRM_INVF + 1], ALU.mult), R=["angA", "prm"], W=["angA"]))
    tbl.append(lambda: P.op("dve", call(V.tensor_single_scalar, angB[:], angA[:], PI / 2, ALU.add), R=["angA"], W=["angB"]))
    kint = posb[:]
    for nm, ang, kflt, dstT in (("c", angB, Sfm, Cfm), ("s", angA, angB, Sfm)):
        ka = "ang" + ("B" if nm == "c" else "A")
        kk = "kf" + nm
        tbl.append(lambda ang=ang, kflt=kflt, ka=ka, kk=kk: P.op("dve", call(V.tensor_single_scalar, kflt[:], ang[:], 1.0 / TWO_PI, ALU.mult), R=[ka, "Cfm_done"], W=[kk]))
        tbl.append(lambda kflt=kflt, kk=kk: P.op("dve", call(V.tensor_copy, kint, kflt[:]), R=[kk], W=["kint"]))
        tbl.append(lambda kflt=kflt, kk=kk: P.op("dve", call(V.tensor_copy, kflt[:], kint), R=["kint"], W=[kk]))
        tbl.append(lambda ang=ang, kflt=kflt, ka=ka, kk=kk: P.op("dve", call(V.scalar_tensor_tensor, ang[:], kflt[:], -TWO_PI, ang[:], ALU.mult, ALU.add), R=[kk, ka], W=[ka]))
        tbl.append(lambda ang=ang, kflt=kflt, ka=ka, kk=kk: P.op("dve", call(V.tensor_scalar, kflt[:], ang[:], PI, -TWO_PI, ALU.is_gt, ALU.mult), R=[ka], W=[kk]))
        tbl.append(lambda ang=ang, kflt=kflt, ka=ka, kk=kk: P.op("dve", call(V.tensor_tensor, ang[:], ang[:], kflt[:], ALU.add), R=[kk, ka], W=[ka]))
        tbl.append(lambda ang=ang, kflt=kflt, ka=ka, kk=kk: P.op("dve", call(V.tensor_scalar, kflt[:], ang[:], -PI, TWO_PI, ALU.is_lt, ALU.mult), R=[ka], W=[kk]))
        tbl.append(lambda ang=ang, kflt=kflt, ka=ka, kk=kk: P.op("dve", call(V.tensor_tensor, ang[:], ang[:], kflt[:], ALU.add), R=[kk, ka], W=[ka]))
        tbl.append(lambda ang=ang, dstT=dstT, ka=ka, nm=nm: P.op("act", call(A.activation, out=dstT[:], in_=ang[:], func=AF.Sin), R=[ka], W=["Cfm_done" if nm == "c" else "Sfm_raw"]))
    tbl.append(lambda: P.op("dve", call(V.tensor_single_scalar, Sfm[:], Sfm[:], prm[:, PRM_SIGN:PRM_SIGN + 1], ALU.mult), R=["Sfm_raw", "prm"], W=["Sfm"]))
    for k, fn in enumerate(tbl):
        at2(-15 + 4.2 * k, fn)

    ev2.sort(key=lambda e: (e[0], e[1]))
    for _, _, fn in ev2:
        fn()
    P.barrier()

    mem.off = wreg0
    WS1 = [mem.alloc(f"WS1_{i}", [128, NKC, 256], F32) for i in range(2)]
    WB1 = [mem.alloc(f"WB1_{i}", [128, NKC, 256], BF16) for i in range(3)]
    permf = mem.alloc("permf", [128, 128], F32)
    permb = mem.alloc("permb", [128, 128], BF16)
    PT1 = [mem.alloc(f"PT1_{i}", [128, 1024], BF16) for i in range(2)]
    NQ1 = 3
    qf1 = [mem.alloc(f"qf1_{i}", [128, 512], F32) for i in range(NQ1)]
    sqb1 = [mem.alloc(f"sqb1_{i}", [128, 512], BF16) for i in range(2)]
    rsb1 = [mem.alloc(f"rsb1_{i}", [128, 512], F32) for i in range(NQ1)]
    qn1 = [mem.alloc(f"qn1_{i}", [128, 512], BF16) for i in range(2)]
    den1 = [mem.alloc(f"den1_{i}", [128, 512], F32) for i in range(2)]
    otmp1 = [mem.alloc(f"otmp1_{i}", [128, 256], F32) for i in range(2)]
    assert mem.off <= wreg1, (mem.off, wreg1)
    wrot["ws"] = 0
    wrot["wb"] = 0
    P.op("pool", call(G.memset, mtmp[:], 0.0), R=["mtmp"], W=["mtmp"])
    P.op("pool", call(G.affine_select, out=mtmp[:], in_=mtmp[:], compare_op=ALU.not_equal, fill=1.0,
                      base=-8, pattern=[[1, 128]], channel_multiplier=-1), R=["mtmp"], W=["mtmp"])
    P.op("dve", call(V.tensor_single_scalar, permf[:], mtmp[:], prm[:, PRM_M1:PRM_M1 + 1], ALU.mult), R=["mtmp", "prm"], W=["permf"])
    P.op("pool", call(G.memset, mtmp[:], 0.0), R=["mtmp"], W=["mtmp"])
    P.op("pool", call(G.affine_select, out=mtmp[:], in_=mtmp[:], compare_op=ALU.not_equal, fill=1.0,
                      base=8, pattern=[[1, 128]], channel_multiplier=-1), R=["mtmp"], W=["mtmp"])
    P.op("dve", call(V.scalar_tensor_tensor, permf[:], mtmp[:], prm[:, PRM_M2:PRM_M2 + 1], permf[:], ALU.mult, ALU.add),
         R=["mtmp", "prm", "permf"], W=["permf"])
    P.op("dve", call(V.tensor_copy, permb[:], permf[:]), R=["permf"], W=["permb"])
    P.op("act", call(A.activation, out=es_bc[:], in_=prm[:, PRM_SINK:PRM_SINK + 16], func=AF.Exp), R=["prm"], W=["es_bc"])
    P.op("dve", call(V.tensor_single_scalar, gq1s[:], prm[:, PRM_GQ1:PRM_GQ1 + 1], 0.125, ALU.mult), R=["prm"], W=["gq1s"])

    P.op("pool", call(G.memset, V1g[:, :, 0:64], 1.0), W=["V1_ones"])
    P.op("pool", call(G.memset, V1g[:, :, 128:192], 1.0), W=["V1_ones"])
    pjb = Rot([4, 5, 6, 7, 0, 1, 2, 3])
    s1b = Rot([0, 2])
    o1b = Rot([4, 5])
    pt1 = Rot([0, 1])
    ev1 = Rot([0, 1])
    qf_rot = Rot(list(range(NQ1)))
    sq_rot = Rot([0, 1])
    qn_rot = Rot([0, 1])
    events1 = []
    seq1 = [0]

    def at1(time, fn):
        seq1[0] += 1
        events1.append((time, seq1[0], fn))

    def sched_group(g, base):
        hold = {}

        def wsrc(kind, gg):
            if kind == "kv":
                return wB_kv[gg], 192
            if kind == "q":
                return wB_q[:, :, gg * 256:(gg + 1) * 256], 256
            if kind == "g":
                return wB_g[:, :, gg * 256:(gg + 1) * 256], 256
            return wB_o[:, 2 * gg:2 * gg + 2, :].rearrange("p k (a n) -> p (k a) n", n=256), 256

        def dma1(kind, gg):
            src, nc_ = wsrc(kind, gg)
            dma1h[(kind, gg)] = w_dma(src, nc_, ws_list=WS1)

        def start():
            if g == 0:
                dma1("kv", 0)
                dma1("q", 0)
            hold["kv"] = w_cast(dma1h[("kv", g)], PRM_GKV, wb_list=WB1)
            hold["q"] = w_cast(dma1h[("q", g)], PRM_GB, wb_list=WB1)
            dma1("g", g)
            dma1("o", g)

        def ld_g():
            hold["g"] = w_cast(dma1h[("g", g)], PRM_GB, wb_list=WB1)
            if g + 1 < 4:
                dma1("kv", g + 1)

        def ld_o():
            hold["o"] = w_cast(dma1h[("o", g)], None, wb_list=WB1)
            if g + 1 < 4:
                dma1("q", g + 1)
        at1(base - 2, start)
        at1(base + 30, ld_g)
        at1(base + 130, ld_o)

        def rope_block(T0, wname, col0, tc, gcol, gkey, dst, dkey):
            st = {}
            tok = slice(tc * 512, (tc + 1) * 512)

            def a():
                b = st["b"] = pjb.next()
                w, kw = hold[wname]
                for kc in range(NKC):
                    P.op("pe", call(T.matmul, bank(b), w[:, kc, col0:col0 + 128], uT1[:, kc, tok], start=(kc == 0), stop=(kc == NKC - 1)),
                         R=[kw] + uT1_keys[tc * 4:tc * 4 + 4], W=[("ps", b)])

            def a2():
                si = st["si"] = qf_rot.next()
                sq = st["sq"] = sq_rot.next()
                P.op("act", call(A.activation, out=qf1[si][:], in_=bank(st["b"]), func=AF.Copy), R=[("ps", st["b"])], W=[("qf1", si)])
                P.op("pool", call(G.tensor_tensor, sqb1[sq][:], qf1[si][:], qf1[si][:], ALU.mult), R=[("qf1", si)], W=[("sqb1", sq)])

            def c_():
                b2 = st["b2"] = pjb.next()
                P.op("pe", call(T.matmul, bank(b2), bd[:], sqb1[st["sq"]][:], start=True, stop=True), R=[("sqb1", st["sq"]), "bd"], W=[("ps", b2)])

            def d1():
                si = st["si"]
                P.op("act", call(A.activation, out=rsb1[si][:], in_=bank(st["b2"]), func=AF.Ln, bias=epsT[:, 0:1]),
                     R=[("ps", st["b2"]), "epsT"], W=[("rsb1", si)])

            def d2():
                si = st["si"]
                P.op("act", call(A.activation, out=rsb1[si][:], in_=rsb1[si][:], func=AF.Exp, scale=-0.5), R=[("rsb1", si)], W=[("rsb1", si)])

            def e():
                si = st["si"]
                qi = st["qi"] = qn_rot.next()
                P.op("dve", call(V.scalar_tensor_tensor, qn1[qi][:], qf1[si][:], gcol, rsb1[si][:], ALU.mult, ALU.mult),
                     R=[("qf1", si), ("rsb1", si), gkey], W=[("qn1", qi)])

            def f():
                b3 = st["b3"] = pjb.next()
                P.op("pe", call(T.matmul, bank(b3), permb[:], qn1[st["qi"]][:], start=True, stop=True), R=[("qn1", st["qi"]), "permb"], W=[("ps", b3)])
                si, qi = st["si"], st["qi"]
                P.op("pool", call(G.tensor_tensor, qf1[si][:], qn1[qi][:], Cfm[:, tok], ALU.mult), R=[("qn1", qi), "Cfm_done", ("qf1", si)], W=[("qf1", si)])

            def g_():
                si, qi, b3 = st["si"], st["qi"], st["b3"]
                P.op("dve", call(V.tensor_tensor, bank(b3), bank(b3), Sfm[:, tok], ALU.mult), R=[("ps", b3), "Sfm"], W=[("ps", b3)])
                P.op("dve", call(V.tensor_tensor, dst, bank(b3), qf1[si][:], ALU.add), R=[("ps", b3), ("qf1", si)], W=[dkey])
            for off, fn in ((0, a), (3, a2), (9, c_), (11, d1), (13, d2), (15, e), (19, f), (23, g_)):
                at1(T0 + off, fn)

        def gate_block(T0, ch, tc):
            st = {}

            def a():
                b = st["b"] = pjb.next()
                w, kw = hold["g"]
                for kc in range(NKC):
                    P.op("pe", call(T.matmul, bank(b), w[:, kc, ch * 128:(ch + 1) * 128], uT1[:, kc, tc * 512:(tc + 1) * 512],
                                    start=(kc == 0), stop=(kc == NKC - 1)), R=[kw] + uT1_keys[tc * 4:tc * 4 + 4], W=[("ps", b)])

            def a2():
                P.op("act", call(A.activation, out=gs1[:, ch, tc * 512:(tc + 1) * 512], in_=bank(st["b"]), func=AF.Copy), R=[("ps", st["b"])], W=[("gs1pre", ch, tc), ("gs1", ch)])
            at1(T0, a)
            at1(T0 + 3, a2)

        def v_block(T0, t):
            st = {}

            def a():
                b = st["b"] = pjb.next()
                w, kw = hold["kv"]
                for kc in range(NKC):
                    P.op("pe", call(T.matmul, bank(b, 64), uT1[:, kc, t * 128:(t + 1) * 128], w[:, kc, 128:192],
                                    start=(kc == 0), stop=(kc == NKC - 1)), R=[kw, ("uT1", t)], W=[("ps", b)])

            def a2():
                P.op("dve", call(V.tensor_copy, V1g[:, t, 64:128], bank(st["b"], 64)), R=[("ps", st["b"]), "V1_ones"], W=[("V1g", t)])
            at1(T0, a)
            at1(T0 + 2, a2)

        blocks = []
        vi_ = 0
        for tc in range(4):
            blocks.append(("k", tc))
            for _ in range(2):
                blocks.append(("v", vi_)); vi_ += 1
        for r in range(8):
            blocks.append(("q", r // 4, r % 4))
            blocks.append(("g", r // 4, r % 4))
            blocks.append(("v", vi_)); vi_ += 1
        for k, blk in enumerate(blocks):
            T0 = base + 1 + 3.6 * k
            if blk[0] == "k":
                rope_block(T0, "kv", 0, blk[1], prm[:, PRM_GK1:PRM_GK1 + 1], "prm", KTs1[:, blk[1] * 512:(blk[1] + 1) * 512], ("KTs", blk[1]))
            elif blk[0] == "q":
                ch, tc = blk[1], blk[2]
                rope_block(T0, "q", ch * 128, tc, gq1s[:, 0:1], "gq1s", QT1[:, ch, tc * 512:(tc + 1) * 512], ("QT1", ch, tc))
            elif blk[0] == "g":
                gate_block(T0, blk[1], blk[2])
            else:
                v_block(T0, blk[1])
        tproj = base + 1 + 3.6 * len(blocks)

        def silu_all():
            for ch in range(2):
                P.op("act", call(A.activation, out=gs1[:, ch, :], in_=gs1[:, ch, :], func=AF.Silu),
                     R=[("gs1pre", ch, tc) for tc in range(4)], W=[("gs1", ch)])
        at1(tproj + 8, silu_all)

        ob_of = {}
        sinfo = {}
        fin_info = {}

        def emit_S1(j):
            nq = 256 if j < NT - 1 else 128
            sb0 = s1b.next()
            qkeys = sorted(set([("QT1", ch, (j * 128) // 512) for ch in range(2)] + [("QT1", ch, (j * 128 + nq - 1) // 512) for ch in range(2)]))
            for ch in range(2):
                co = ch * 256
                for half in range(2):
                    p0 = half * 64
                    sbk = sb0 + half
                    P.op("pe", call(T.matmul, bank(sbk, nq, c0=co), KTs1[p0:p0 + 64, j * 128:(j + 1) * 128], QT1[p0:p0 + 64, ch, j * 128:j * 128 + nq],
                                    start=(ch == 0), stop=False, skip_group_check=True),
                         R=[("KTs", j // 4)] + qkeys, W=[("ps", sbk)])
                for half in range(2):
                    sbk = sb0 + half
                    P.op("pe", call(T.matmul, bank(sbk, 128, c0=co), ident[:], maskC[:], start=False, stop=(nq == 128), skip_group_check=True),
                         R=["ident", "maskC"], W=[("ps", sbk)])
                    if nq == 256:
                        P.op("pe", call(T.matmul, bank(sbk, 128, c0=co + 128), ident[:], maskW[:], start=False, stop=True, skip_group_check=True),
                             R=["ident", "maskW"], W=[("ps", sbk)])
            pi = pt1.next()
            if nq == 256:
                P.op("act", call(A.activation, out=PT1[pi][:], in_=psum[:, sb0 * 512:sb0 * 512 + 1024], func=AF.Exp),
                     R=[("ps", sb0), ("ps", sb0 + 1)], W=[("PT1", pi)])
            else:
                for half in range(2):
                    for ch in range(2):
                        pos = half * 2 + ch
                        P.op("act", call(A.activation, out=PT1[pi][:, pos * 256:pos * 256 + 128], in_=bank(sb0 + half, 128, c0=ch * 256), func=AF.Exp),
                             R=[("ps", sb0), ("ps", sb0 + 1)], W=[("PT1", pi)])
            sinfo[j] = pi

        def emit_PV1(j):
            pi = sinfo[j]
            pt4 = PT1[pi][:, :].rearrange("p (h q) -> p h q", q=256)
            for qi, qoff in ((j, 0), (j + 1, 128)):
                if qi >= NT:
                    continue
                first = (qi == j + 1) or (j == 0)
                last = (qi == j)
                if first:
                    ob_of[qi] = o1b.next()
                ob = ob_of[qi]
                for par in range(2):
                    lc = 64 if par == 0 else 0
                    P.op("pe", call(T.matmul, bank(ob, 256, c0=par * 256), V1g[:, j, lc:lc + 128], pt4[:, 2 * par:2 * par + 2, qoff:qoff + 128],
                                    start=(first and par == 0), stop=(last and par == 1), skip_group_check=True),
                         R=[("PT1", pi), ("V1g", j), "V1_ones"], W=[("ps", ob)])
                if not last:
                    continue
                ei = ev1.next()
                for par in range(2):
                    o0, s0 = (0, 64) if par == 0 else (64, 0)
                    P.op("dve", call(V.tensor_tensor,
                                     den1[ei][s0:s0 + 64, 0:256].rearrange("p (h q) -> p h q", q=128),
                                     bank(ob, 256, s0, s0 + 64, c0=par * 256).rearrange("p (h q) -> p h q", q=128),
                                     es_bc[s0:s0 + 64, 4 * g + par:4 * g + 4:2].unsqueeze(2).to_broadcast([64, 2, 128]), ALU.add),
                         R=[("ps", ob), "es_bc"], W=[("den1", ei, par)])
                P.op("act", call(A.activation, out=den1[ei][:, 0:256], in_=den1[ei][:, 0:256], func=AF.Ln),
                     R=[("den1", ei, 0), ("den1", ei, 1)], W=[("den1", ei, 0), ("den1", ei, 1)])
                for par in range(2):
                    o0, s0 = (0, 64) if par == 0 else (64, 0)
                    P.op("act", call(A.activation, out=den1[ei][o0:o0 + 64, 256:512], in_=den1[ei][s0:s0 + 64, 0:256], func=AF.Exp, scale=-1.0),
                         R=[("den1", ei, par)], W=[("den1b", ei, par)])
                fin_info[qi] = (ei, ob)

        def emit_fin1(qi):
            ei, ob = fin_info[qi]
            for par in range(2):
                o0, s0 = (0, 64) if par == 0 else (64, 0)
                P.op("dve", call(V.tensor_tensor, otmp1[ei][o0:o0 + 64, 0:256], bank(ob, 256, o0, o0 + 64, c0=par * 256),
                                 den1[ei][o0:o0 + 64, 256:512], ALU.mult),
                     R=[("ps", ob), ("den1b", ei, par)], W=[("otmp1", ei, par)])
            for par in range(2):
                o0, s0 = (0, 64) if par == 0 else (64, 0)
                P.op("pool", call(G.tensor_tensor, og1[o0:o0 + 64, :, qi * 128:(qi + 1) * 128],
                                  otmp1[ei][o0:o0 + 64, 0:256].rearrange("p (h q) -> p h q", q=128),
                                  gs1[o0:o0 + 64, :, qi * 128:(qi + 1) * 128], ALU.mult),
                     R=[("otmp1", ei, par), ("gs1", 0), ("gs1", 1)], W=[("og1", qi, par)])

        def emit_oproj(t):
            w, kw = hold["o"]
            for hf in range(2):
                b = 6 + hf
                for k in range(2):
                    P.op("pe", call(T.matmul, bank(b), og1[:, k, t * 128:(t + 1) * 128],
                                    w[:, k * 4 + 2 * hf:k * 4 + 2 * hf + 2, :].rearrange("p a n -> p (a n)"), start=(k == 0), stop=(k == 1)),
                         R=[kw, ("og1", t, 0), ("og1", t, 1)], W=[("ps", b)])
                P.op("dve", call(V.tensor_tensor, h1[:, t, hf * 512:(hf + 1) * 512], bank(b), h1[:, t, hf * 512:(hf + 1) * 512], ALU.add),
                     R=[("ps", b), ("h1", t, hf)], W=[("h1", t, hf)])

        ta = tproj + 14
        for j in range(NT):
            at1(ta + 6 * j, lambda j=j: emit_S1(j))
            at1(ta + 6 * (j + 1) + 0.1, lambda j=j: emit_PV1(j))
            at1(ta + 6 * (j + 1) + 5, lambda j=j: emit_fin1(j))
            at1(ta + 6 * (j + 3) + 0.5, lambda j=j: emit_oproj(j))
        return ta + 6 * (NT + 3) + 6

    tb = 0.0
    dma1h = {}
    for g in range(4):
        tb = sched_group(g, tb) + 4
    events1.sort(key=lambda e: (e[0], e[1]))
    for _, _, fn in events1:
        fn()

    for t in range(NT):
        P.dma(call(SP.dma_start, out=out_d[t * 128:(t + 1) * 128, :], in_=h1[:, t, :]), R=[("h1", t, 0), ("h1", t, 1)], W=[("out", t)], sem="outst")
    P.barrier()
    P.emit()
    return nc


def _pk(w):
    n = w.shape[1]
    return np.ascontiguousarray(w.reshape(NKC, 128, n).transpose(1, 0, 2))


def _prep_shared(positions, norm_a_g, w_in_a, b_forget, qnorm_a_g, knorm_a_g, w_out_a, kv_norm_g, w_kv, knorm_b_g,
                 norm_b_g, w_in_b, qnorm_b_g, sinks, w_out_b):
    f32 = np.float32
    wa = np.asarray(w_in_a, f32)[0]
    hp = []
    for c in range(8):
        cols = np.concatenate([np.arange(c * 128, c * 128 + 128), 1024 + np.arange(c * 128, c * 128 + 128),
                               3088 + np.arange(c * 128, c * 128 + 128), 2048 + np.arange(c * 128, c * 128 + 128)])
        hp.append(_pk(wa[:, cols]))
    wkv = np.asarray(w_kv, f32)
    kvg = []
    for g in range(4):
        cols = np.concatenate([np.arange(g * 64, g * 64 + 64), np.arange(g * 64, g * 64 + 64), 256 + np.arange(g * 64, g * 64 + 64)])
        kvg.append(_pk(wkv[:, cols]))
    wb = np.asarray(w_in_b, f32)[0]
    prm = np.zeros((128, PRM_N), f32)
    prm[:, PRM_GA:PRM_GA + 8] = np.asarray(norm_a_g, f32)[0].reshape(NKC, 128).T
    prm[:, PRM_GKV:PRM_GKV + 8] = np.asarray(kv_norm_g, f32).reshape(NKC, 128).T
    prm[:, PRM_GB:PRM_GB + 8] = np.asarray(norm_b_g, f32)[0].reshape(NKC, 128).T
    prm[:, PRM_GQ0] = np.tile(np.asarray(qnorm_a_g, f32)[0], 2)
    prm[:, PRM_GK0] = np.tile(np.asarray(knorm_a_g, f32)[0], 2)
    prm[0:16, PRM_BF] = np.asarray(b_forget, f32)[0]
    prm[:, PRM_GQ1] = np.tile(np.asarray(qnorm_b_g, f32)[0], 2)
    prm[:, PRM_GK1] = np.tile(np.asarray(knorm_b_g, f32), 2)
    dd = np.arange(128) % 64
    inv_freq = (np.float32(ROPE_THETA) ** (-(2.0 * (dd % 8)) / 16.0)).astype(f32)
    prm[:, PRM_INVF] = np.where(dd < 16, inv_freq, 0.0)
    prm[:, PRM_SIGN] = np.where(dd < 8, -1.0, np.where(dd < 16, 1.0, 0.0))
    prm[:, PRM_M1] = (dd < 8).astype(f32)
    prm[:, PRM_M2] = ((dd >= 8) & (dd < 16)).astype(f32)
    prm[:, PRM_SINK:PRM_SINK + 16] = np.asarray(sinks, f32)[0][None, :]
    return {
        "pos": np.ascontiguousarray(np.asarray(positions, np.int32).reshape(NT, 128).T),
        "posb": np.ascontiguousarray(np.broadcast_to(np.asarray(positions, np.int32)[None, :], (128, S))),
        "prm": prm,
        "wA_hp": np.stack(hp),
        "wA_f": _pk(wa[:, 3072:3088]),
        "wA_o": _pk(np.asarray(w_out_a, f32)[0]),
        "wB_kv": np.stack(kvg),
        "wB_q": _pk(wb[:, 0:1024]),
        "wB_g": _pk(wb[:, 1024:2048]),
        "wB_o": _pk(np.asarray(w_out_b, f32)[0]),
    }


_NC_CACHE = {}


def kernel(x, positions, norm_a_g, w_in_a, b_forget, qnorm_a_g, knorm_a_g, w_out_a, kv_norm_g, w_kv, knorm_b_g,
           norm_b_g, w_in_b, qnorm_b_g, sinks, w_out_b, _dbg=None):
    x = np.asarray(x, np.float32)
    shared = _prep_shared(positions, norm_a_g, w_in_a, b_forget, qnorm_a_g, knorm_a_g, w_out_a, kv_norm_g, w_kv,
                          knorm_b_g, norm_b_g, w_in_b, qnorm_b_g, sinks, w_out_b)
    nb = x.shape[0]
    nc = build(_dbg)
    in_maps = []
    for b in range(nb):
        m = dict(shared)
        m["x"] = np.ascontiguousarray(x[b])
        in_maps.append(m)
    res = run_bass_kernel_spmd(nc, in_maps, core_ids=list(range(nb)))
    return np.stack([np.asarray(r["out"], np.float32) for r in res.results], axis=0)
```
